# Optimizing a Trainium2 kernel written in Bass

```python
import jax, jax.numpy as jnp
from jax import lax
import numpy as np

D_MODEL = 1024
BATCH = 4
SEQ = 4096
DEPTH = 2

GRID_W = 64
CTX_LEN = 256
D_MIX = D_MODEL
N_GROUPS = 4
D_GROUP = D_MIX // N_GROUPS
CONV_WIDTH = 31
FNET_HEADS = 4
FNET_HEAD_DIM = D_GROUP // FNET_HEADS
NA_HEADS = 4
NA_HEAD_DIM = D_GROUP // NA_HEADS
NA_WIN_ROWS = 8
NA_WIN_COLS = 16
NA_QBLOCK_COLS = 16
NA_KBLOCK_COLS = NA_QBLOCK_COLS + NA_WIN_COLS
SSM_GROUP_CH = 16
SSM_GROUPS = D_GROUP // SSM_GROUP_CH
SSM_STATE = 64
D_FF = 4 * D_MODEL
EPS = 1e-6
COLS_CONV = 2 * D_GROUP
COLS_FNET = D_GROUP
COLS_NA = 3 * D_GROUP
COLS_SSM = D_GROUP
IN_COLS = COLS_CONV + COLS_FNET + COLS_NA + COLS_SSM

kernel_name = "hybrid_parallel_group_dit_block"


def rmsnorm(x, g):
    xf = x.astype(jnp.float32)
    y = xf * lax.rsqrt(jnp.mean(xf * xf, axis=-1, keepdims=True) + EPS)
    return (y * g.astype(jnp.float32)).astype(x.dtype)


def layernorm(x, g, b):
    xf = x.astype(jnp.float32)
    mu = jnp.mean(xf, axis=-1, keepdims=True)
    var = jnp.mean(jnp.square(xf - mu), axis=-1, keepdims=True)
    y = (xf - mu) * lax.rsqrt(var + EPS)
    return (y * g.astype(jnp.float32) + b.astype(jnp.float32)).astype(x.dtype)


def modulate(h, shift, scale):
    return h * (1 + scale) + shift


def group_rmsnorm(y, g):
    shp = y.shape
    yf = y.astype(jnp.float32).reshape(shp[:-1] + (N_GROUPS, D_GROUP))
    yf = yf * lax.rsqrt(jnp.mean(yf * yf, axis=-1, keepdims=True) + EPS)
    return (yf.reshape(shp) * g.astype(jnp.float32)).astype(y.dtype)


def split_cols(p):
    o1 = COLS_CONV
    o2 = o1 + COLS_FNET
    o3 = o2 + COLS_NA
    return p[..., :o1], p[..., o1:o2], p[..., o2:o3], p[..., o3:]


def conformer_conv(u, w_dw, b_dw, ln_g, ln_b, w_pw, b_pw):
    a, g = jnp.split(u, 2, axis=-1)
    v = a * jax.nn.sigmoid(g)
    v = lax.conv_general_dilated(
        v, w_dw[:, None, :], window_strides=(1,),
        padding=[(CONV_WIDTH // 2, CONV_WIDTH // 2)],
        dimension_numbers=("NWC", "WIO", "NWC"),
        feature_group_count=D_GROUP) + b_dw
    v = jax.nn.silu(layernorm(v, ln_g, ln_b))
    return v @ w_pw + b_pw


def fourier_mix(u, w_f, b_f):
    bsz, L, _ = u.shape
    uh = u.astype(jnp.float32).reshape(bsz, L, FNET_HEADS, FNET_HEAD_DIM)
    f = jnp.fft.fftn(uh, axes=(1, 3), norm="ortho").real
    return f.reshape(bsz, L, D_GROUP).astype(u.dtype) @ w_f + b_f


def neighbourhood_attention(q, k, v, kc, vc, rpb):
    bsz, L = q.shape[0], q.shape[1]
    rows = L // GRID_W
    wr = min(NA_WIN_ROWS, rows)
    ncb = GRID_W // NA_QBLOCK_COLS
    scale = NA_HEAD_DIM ** -0.5
    grid = lambda t: t.reshape(bsz, rows, GRID_W, NA_HEADS, NA_HEAD_DIM)
    qg, kg, vg = grid(q), grid(k), grid(v)
    r = jnp.arange(rows)
    row_start = jnp.clip(r - wr // 2, 0, rows - wr)
    row_idx = row_start[:, None] + jnp.arange(wr)
    j = jnp.arange(ncb)
    kcol_start = jnp.clip(j * NA_QBLOCK_COLS - NA_WIN_COLS // 2, 0, GRID_W - NA_KBLOCK_COLS)
    col_idx = kcol_start[:, None] + jnp.arange(NA_KBLOCK_COLS)
    row_sel = row_idx[:, None, :, None]
    col_sel = col_idx[None, :, None, :]
    kb = kg[:, row_sel, col_sel]
    vb = vg[:, row_sel, col_sel]
    qb = qg.reshape(bsz, rows, ncb, NA_QBLOCK_COLS, NA_HEADS, NA_HEAD_DIM)
    qcol = j[:, None] * NA_QBLOCK_COLS + jnp.arange(NA_QBLOCK_COLS)
    cstart = jnp.clip(qcol - NA_WIN_COLS // 2, 0, GRID_W - NA_WIN_COLS)
    kcol = col_idx[:, None, :]
    in_win = (kcol >= cstart[:, :, None]) & (kcol < cstart[:, :, None] + NA_WIN_COLS)
    dr = row_idx - r[:, None] + (NA_WIN_ROWS - 1)
    dc = jnp.clip(kcol - qcol[:, :, None] + (NA_WIN_COLS - 1), 0, 2 * NA_WIN_COLS - 2)
    bias = rpb[:, dr[:, None, None, :, None], dc[None, :, :, None, :]]
    s_loc = jnp.einsum("brjqhd,brjikhd->bhrjqik", qb, kb,
                       preferred_element_type=jnp.float32) * scale
    s_loc = s_loc + bias[None].astype(jnp.float32)
    s_loc = jnp.where(in_win[:, :, None, :], s_loc, jnp.float32(-1e30))
    s_ctx = jnp.einsum("brjqhd,bchd->bhrjqc", qb, kc,
                       preferred_element_type=jnp.float32) * scale
    n_loc = wr * NA_KBLOCK_COLS
    s = jnp.concatenate([s_loc.reshape(s_loc.shape[:5] + (n_loc,)), s_ctx], axis=-1)
    p = jax.nn.softmax(s, axis=-1).astype(v.dtype)
    p_loc = p[..., :n_loc].reshape(s_loc.shape)
    p_ctx = p[..., n_loc:]
    out = (jnp.einsum("bhrjqik,brjikhd->brjqhd", p_loc, vb)
           + jnp.einsum("bhrjqc,bchd->brjqhd", p_ctx, vc))
    return out.reshape(bsz, L, D_GROUP)


def context_attention(q, k, v):
    s = jnp.einsum("bqhd,bkhd->bhqk", q, k, preferred_element_type=jnp.float32) * NA_HEAD_DIM ** -0.5
    p = jax.nn.softmax(s, axis=-1).astype(v.dtype)
    out = jnp.einsum("bhqk,bkhd->bqhd", p, v)
    return out.reshape(q.shape[0], q.shape[1], D_GROUP)


def _scan_op(e1, e2):
    a1, b1 = e1
    a2, b2 = e2
    return a1 * a2, a2 * b1 + b2


def diag_scan(a_bar, bu, reverse):
    a = jnp.broadcast_to(a_bar, bu.shape)
    _, h = lax.associative_scan(_scan_op, (a, bu), axis=1, reverse=reverse)
    return h


def s5_bidirectional(u_x, u_c, a_re, a_im, log_dt, b_re, b_im, c_re, c_im, d_skip,
                     w_glu, b_glu, with_ctx_out):
    f32 = jnp.float32
    bsz, L, _ = u_x.shape
    n_ctx = u_c.shape[1]
    ux = u_x.astype(f32).reshape(bsz, L, SSM_GROUPS, SSM_GROUP_CH)
    uc = u_c.astype(f32).reshape(bsz, n_ctx, SSM_GROUPS, SSM_GROUP_CH)
    d = d_skip.astype(f32).reshape(SSM_GROUPS, SSM_GROUP_CH)
    uxc = ux.astype(jnp.complex64)
    ucc = uc.astype(jnp.complex64)
    y_x = d * ux
    y_c = d * uc
    for dirn, reverse in ((0, False), (1, True)):
        lam = lax.complex(a_re[dirn].astype(f32), a_im[dirn].astype(f32))
        dt = jnp.exp(log_dt[dirn].astype(f32))[:, None]
        a_bar = jnp.exp(lam * dt)
        b_bar = ((a_bar - 1) / lam)[..., None] * lax.complex(
            b_re[dirn].astype(f32), b_im[dirn].astype(f32))
        c_mat = lax.complex(c_re[dirn].astype(f32), c_im[dirn].astype(f32))
        h_c = diag_scan(a_bar, jnp.einsum("blgm,gpm->blgp", ucc, b_bar), reverse)
        h0 = h_c[:, 0] if reverse else h_c[:, -1]
        bu_x = jnp.einsum("blgm,gpm->blgp", uxc, b_bar)
        edge = -1 if reverse else 0
        bu_x = bu_x.at[:, edge].add(a_bar * h0)
        h_x = diag_scan(a_bar, bu_x, reverse)
        y_x = y_x + jnp.einsum("blgp,gmp->blgm", h_x, c_mat).real
        if with_ctx_out:
            y_c = y_c + jnp.einsum("blgp,gmp->blgm", h_c, c_mat).real

    def glu(y):
        z = y.reshape(y.shape[0], y.shape[1], D_GROUP).astype(u_x.dtype) @ w_glu + b_glu
        a, g = jnp.split(z, 2, axis=-1)
        return a * jax.nn.sigmoid(g)

    return glu(y_x), (glu(y_c) if with_ctx_out else None)


def squared_relu_mlp(h, w1, b1, w2, b2):
    return jnp.square(jax.nn.relu(h @ w1 + b1)) @ w2 + b2


def setup_inputs(seed: int = 0) -> dict:
    key = jax.random.key(seed)
    ks = iter(jax.random.split(key, 48))
    nrm = lambda shape, s: s * jax.random.normal(next(ks), shape, jnp.float32)
    Ld = DEPTH
    G, P, M = SSM_GROUPS, SSM_STATE, SSM_GROUP_CH
    n = jnp.arange(P, dtype=jnp.float32)
    return {
        "x": nrm((BATCH, SEQ, D_MODEL), 1.0),
        "c": nrm((BATCH, D_MODEL), 1.0),
        "ctx": nrm((BATCH, CTX_LEN, D_MODEL), 1.0),
        "c_ctx": nrm((D_MODEL,), 1.0),
        "w_mod": nrm((Ld, D_MODEL, 6 * D_MODEL), D_MODEL ** -0.5),
        "b_mod": nrm((Ld, 6 * D_MODEL), 0.02),
        "g_norm_mix": 1.0 + nrm((Ld, D_MODEL), 0.02),
        "w_in": nrm((Ld, D_MODEL, IN_COLS), D_MODEL ** -0.5),
        "conv_w_dw": nrm((Ld, CONV_WIDTH, D_GROUP), CONV_WIDTH ** -0.5),
        "conv_b_dw": nrm((Ld, D_GROUP), 0.02),
        "conv_ln_g": 1.0 + nrm((Ld, D_GROUP), 0.02),
        "conv_ln_b": nrm((Ld, D_GROUP), 0.02),
        "conv_w_pw": nrm((Ld, D_GROUP, D_GROUP), D_GROUP ** -0.5),
        "conv_b_pw": nrm((Ld, D_GROUP), 0.02),
        "fnet_w": nrm((Ld, D_GROUP, D_GROUP), D_GROUP ** -0.5),
        "fnet_b": nrm((Ld, D_GROUP), 0.02),
        "na_rpb": nrm((Ld, NA_HEADS, 2 * NA_WIN_ROWS - 1, 2 * NA_WIN_COLS - 1), 0.1),
        "ssm_a_re": -0.5 + nrm((Ld, 2, G, P), 0.01),
        "ssm_a_im": jnp.pi * n + nrm((Ld, 2, G, P), 0.01),
        "ssm_log_dt": jax.random.uniform(next(ks), (Ld, 2, G), jnp.float32,
                                         minval=float(np.log(1e-3)), maxval=float(np.log(1e-1))),
        "ssm_b_re": nrm((Ld, 2, G, P, M), (2 * M) ** -0.5),
        "ssm_b_im": nrm((Ld, 2, G, P, M), (2 * M) ** -0.5),
        "ssm_c_re": nrm((Ld, 2, G, M, P), P ** -0.5),
        "ssm_c_im": nrm((Ld, 2, G, M, P), P ** -0.5),
        "ssm_d": nrm((Ld, D_GROUP), 0.5),
        "ssm_w_glu": nrm((Ld, D_GROUP, 2 * D_GROUP), D_GROUP ** -0.5),
        "ssm_b_glu": nrm((Ld, 2 * D_GROUP), 0.02),
        "g_group": 1.0 + nrm((Ld, D_MIX), 0.02),
        "w_out": nrm((Ld, D_MIX, D_MODEL), D_MIX ** -0.5),
        "b_out": nrm((Ld, D_MODEL), 0.02),
        "g_norm_mlp": 1.0 + nrm((Ld, D_MODEL), 0.02),
        "w_ff1": nrm((Ld, D_MODEL, D_FF), D_MODEL ** -0.5),
        "b_ff1": nrm((Ld, D_FF), 0.02),
        "w_ff2": nrm((Ld, D_FF, D_MODEL), D_FF ** -0.5),
        "b_ff2": nrm((Ld, D_MODEL), 0.02),
        "g_final": 1.0 + nrm((D_MODEL,), 0.02),
    }


def reference(x, c, ctx, c_ctx, w_mod, b_mod, g_norm_mix, w_in,
              conv_w_dw, conv_b_dw, conv_ln_g, conv_ln_b, conv_w_pw, conv_b_pw,
              fnet_w, fnet_b, na_rpb,
              ssm_a_re, ssm_a_im, ssm_log_dt, ssm_b_re, ssm_b_im, ssm_c_re, ssm_c_im,
              ssm_d, ssm_w_glu, ssm_b_glu,
              g_group, w_out, b_out, g_norm_mlp, w_ff1, b_ff1, w_ff2, b_ff2, g_final):
    bsz, L, _ = x.shape
    n_ctx = ctx.shape[1]
    for l in range(DEPTH):
        ctx_out = l < DEPTH - 1
        mod_x = jax.nn.silu(c) @ w_mod[l] + b_mod[l]
        mod_c = jax.nn.silu(c_ctx) @ w_mod[l] + b_mod[l]
        sh1, sc1, gt1, sh2, sc2, gt2 = jnp.split(mod_x[:, None, :], 6, axis=-1)
        ch1, cc1, cg1, ch2, cc2, cg2 = jnp.split(mod_c, 6)

        px = modulate(rmsnorm(x, g_norm_mix[l]), sh1, sc1) @ w_in[l]
        pc = modulate(rmsnorm(ctx, g_norm_mix[l]), ch1, cc1) @ w_in[l]
        conv_x, fnet_x, qkv_x, ssm_x = split_cols(px)
        conv_c, fnet_c, qkv_c, ssm_c = split_cols(pc)
        qkv_x = qkv_x.reshape(bsz, L, 3, NA_HEADS, NA_HEAD_DIM)
        qkv_c = qkv_c.reshape(bsz, n_ctx, 3, NA_HEADS, NA_HEAD_DIM)

        y_conv_x = conformer_conv(conv_x, conv_w_dw[l], conv_b_dw[l], conv_ln_g[l],
                                  conv_ln_b[l], conv_w_pw[l], conv_b_pw[l])
        y_fnet_x = fourier_mix(fnet_x, fnet_w[l], fnet_b[l])
        y_na_x = neighbourhood_attention(qkv_x[:, :, 0], qkv_x[:, :, 1], qkv_x[:, :, 2],
                                         qkv_c[:, :, 1], qkv_c[:, :, 2], na_rpb[l])
        y_ssm_x, y_ssm_c = s5_bidirectional(
            ssm_x, ssm_c, ssm_a_re[l], ssm_a_im[l], ssm_log_dt[l], ssm_b_re[l], ssm_b_im[l],
            ssm_c_re[l], ssm_c_im[l], ssm_d[l], ssm_w_glu[l], ssm_b_glu[l], ctx_out)
        mix_x = jnp.concatenate([y_conv_x, y_fnet_x, y_na_x, y_ssm_x], axis=-1)
        out_x = group_rmsnorm(mix_x, g_group[l]) @ w_out[l] + b_out[l]
        x_mid = x + gt1 * out_x

        if ctx_out:
            y_conv_c = conformer_conv(conv_c, conv_w_dw[l], conv_b_dw[l], conv_ln_g[l],
                                      conv_ln_b[l], conv_w_pw[l], conv_b_pw[l])
            y_fnet_c = fourier_mix(fnet_c, fnet_w[l], fnet_b[l])
            y_na_c = context_attention(qkv_c[:, :, 0], qkv_c[:, :, 1], qkv_c[:, :, 2])
            mix_c = jnp.concatenate([y_conv_c, y_fnet_c, y_na_c, y_ssm_c], axis=-1)
            out_c = group_rmsnorm(mix_c, g_group[l]) @ w_out[l] + b_out[l]
            ctx_mid = ctx + cg1 * out_c

        hx = modulate(rmsnorm(x_mid, g_norm_mlp[l]), sh2, sc2)
        x = x_mid + gt2 * squared_relu_mlp(hx, w_ff1[l], b_ff1[l], w_ff2[l], b_ff2[l])
        if ctx_out:
            hc = modulate(rmsnorm(ctx_mid, g_norm_mlp[l]), ch2, cc2)
            ctx = ctx_mid + cg2 * squared_relu_mlp(hc, w_ff1[l], b_ff1[l], w_ff2[l], b_ff2[l])
    return rmsnorm(x, g_final)
```

```python
import numpy as np
from contextlib import ExitStack
import concourse.bass as bass
import concourse.mybir as mybir
from concourse.bass_utils import run_bass_kernel_spmd

F32 = mybir.dt.float32
BF16 = mybir.dt.bfloat16
AF = mybir.ActivationFunctionType
ALU = mybir.AluOpType
AX = mybir.AxisListType

N_DMA_SEMS = 40


class Prog:
    ENGS = ["tensor", "vector", "scalar", "gpsimd", "sync"]

    def __init__(self, nc, es):
        self.nc = nc
        self.es = es
        self.ops = {e: [] for e in self.ENGS}
        self.seq = {e: 0 for e in self.ENGS}
        self.sem = {e: es.enter_context(nc.semaphore("s_" + e)) for e in self.ENGS[:4]}
        self.dsem = [es.enter_context(nc.semaphore("d_%d" % i)) for i in range(N_DMA_SEMS)]
        self.dcnt = [0] * N_DMA_SEMS
        self.drr = 0
        self.last_w = {}
        self.readers = {}
        self.waited = {e: {} for e in self.ENGS}
        self.nbuf = 0

    def sb(self, name, shape, dtype):
        return self.es.enter_context(self.nc.sbuf_tensor("sb_" + name, list(shape), dtype))

    def ps(self, name, shape, dtype=F32):
        return self.es.enter_context(self.nc.psum_tensor("ps_" + name, list(shape), dtype))

    def _deps(self, eng, r, w):
        deps = set()
        for k in r:
            if k in self.last_w:
                deps.add(self.last_w[k])
        for k in w:
            if k in self.last_w:
                deps.add(self.last_w[k])
            for t in self.readers.get(k, ()):
                deps.add(t)
        waits = []
        for (sid, val, peng) in sorted(deps, key=lambda t: (str(t[0]), t[1])):
            if peng == eng and eng == "tensor":
                continue
            wd = self.waited[eng]
            if wd.get(sid, 0) >= val:
                continue
            wd[sid] = val
            waits.append((sid, val))
        return waits

    def _commit(self, tok, r, w):
        for k in r:
            self.readers.setdefault(k, []).append(tok)
        for k in w:
            self.last_w[k] = tok
            self.readers[k] = []

    def _semh(self, sid):
        return self.sem[sid] if isinstance(sid, str) else self.dsem[sid]

    def op(self, eng, fn, r=(), w=()):
        waits = self._deps(eng, r, w)
        self.seq[eng] += 1
        tok = (eng, self.seq[eng], eng)
        self.ops[eng].append((fn, waits, (eng, 1)))
        self._commit(tok, r, w)

    def dma(self, eng, out_ap, in_ap, r=(), w=(), **kw):
        waits = self._deps(eng, r, w)
        j = self.drr
        self.drr = (self.drr + 1) % N_DMA_SEMS
        if self.dcnt[j] > 0 and self.waited[eng].get(j, 0) < 16 * self.dcnt[j]:
            self.waited[eng][j] = 16 * self.dcnt[j]
            waits.append((j, 16 * self.dcnt[j]))
        self.dcnt[j] += 1
        tok = (j, 16 * self.dcnt[j], None)
        self.ops[eng].append((lambda e: e.dma_start(out=out_ap, in_=in_ap, **kw), waits, (j, 16)))
        self._commit(tok, r, w)

    def emit(self):
        nc = self.nc
        fin = [(j, 16 * c) for j, c in enumerate(self.dcnt) if c > 0]
        with nc.Block() as block:
            for eng in self.ENGS:
                ops = self.ops[eng]

                def body(e, ops=ops, eng=eng):
                    for (fn, waits, inc) in ops:
                        for (sid, val) in waits:
                            e.wait_ge(self._semh(sid), val)
                        ins = fn(e)
                        ins.then_inc(self._semh(inc[0]), inc[1])
                    if eng == "sync":
                        for (sid, val) in fin:
                            e.wait_ge(self._semh(sid), val)
                getattr(block, eng)(body)
        self.ops = {e: [] for e in self.ENGS}


D = 1024
NT_OWN = 2048
NT_HALO = 256
NT_ALL = 4096
NCTX = 256
EPS = 1e-6
TA = 256


class K:
    pass


def load_w_bf16(P, dst, w_dram, kc_n, ncols, stage, tag, col0=0, cast_engs=("gpsimd", "vector")):
    wv = w_dram.rearrange("(kc p) n -> p kc n", p=128)
    blk = max(1, 2048 // (kc_n))
    blk = min(blk, ncols)
    i = 0
    c = 0
    while c < ncols:
        n = min(blk, ncols - c)
        st = stage[i % len(stage)]
        skey = "stgbuf%d" % id(st)
        stv = st[:, 0:kc_n * n].rearrange("p (kc n) -> p kc n", kc=kc_n)
        P.dma("sync", stv, wv[:, :, col0 + c:col0 + c + n], r=[], w=[skey])
        eng = cast_engs[i % len(cast_engs)]
        d = dst[:, :, c:c + n]
        P.op(eng, (lambda e, d=d, stv=stv: e.tensor_copy(d, stv)), r=[skey], w=[tag])
        c += n
        i += 1


def build_layer(ctx_out, final_norm, dbg=None):
    nc = bass.Bass("TRN2", target_bir_lowering=False)
    dt = nc.dram_tensor

    def din(name, shape, dtype=F32):
        return dt(name, list(shape), dtype, kind="ExternalInput").ap()

    def dout(name, shape, dtype=F32):
        return dt(name, list(shape), dtype, kind="ExternalOutput").ap()

    I = {}
    I["xT"] = din("xT", [D, NT_ALL])
    I["ctxT"] = din("ctxT", [D, NCTX])
    I["cT"] = din("cT", [128, 16])
    I["w_mod"] = din("w_mod", [D, 6 * D])
    I["b_mod2"] = din("b_mod2", [2, 6 * D])
    I["gvecs"] = din("gvecs", [128, 8, 4])
    I["w_in"] = din("w_in", [D, 1792])
    I["w_out"] = din("w_out", [D, D])
    I["w_ff1"] = din("w_ff1", [D, 4 * D])
    I["b_ff1"] = din("b_ff1", [128, 32])
    I["w_ff2"] = din("w_ff2", [4 * D, D])
    I["g_group"] = din("g_group", [128, 8])
    I["g_final"] = din("g_final", [128, 8])
    I["conv_wT"] = din("conv_wT", [128, 2, 31])
    I["conv_vec"] = din("conv_vec", [128, 2, 4])
    I["conv_w_pw"] = din("conv_w_pw", [256, 256])
    I["natab"] = din("natab", [3, 128, 4, 640])
    I["sp_a"] = din("sp_a", [128, 3, 16])
    I["sp_b"] = din("sp_b", [128, 2, 16, 16])
    I["sp_c"] = din("sp_c", [128, 2, 16, 16])
    I["sp_d"] = din("sp_d", [128, 16])
    I["mskF"] = din("mskF", [128, 128])
    I["mskB"] = din("mskB", [128, 128])
    I["sel"] = din("sel", [128, 64, 128], BF16)
    I["selT"] = din("selT", [128, 64, 128], BF16)
    I["ssm_w_glu"] = din("ssm_w_glu", [256, 512])
    I["ssm_b_glu"] = din("ssm_b_glu", [128, 4])
    I["fnet_cs"] = din("fnet_cs", [128, 2, 512], BF16)
    I["fnet_w"] = din("fnet_w", [256, 256])
    I["fnet_b"] = din("fnet_b", [128, 2])
    I["dft"] = din("dft", [4, 8, 128, 4, 2, 512], BF16)
    I["dftc"] = din("dftc", [128, 2, 2, 256], BF16)
    if dbg == "mixin":
        I["mix_dbg"] = din("mix_dbg", [D, NT_OWN + NCTX])
    O = {}
    O["xoT"] = dout("xoT", [D, NT_OWN])
    if ctx_out:
        O["ctxoT"] = dout("ctxoT", [D, NCTX])
    if dbg:
        O["dbg_px"] = dout("dbg_px", [1792, 512])
        O["dbg_mod"] = dout("dbg_mod", [128, 96])
        O["dbg_mix"] = dout("dbg_mix", [D, NT_OWN + NCTX], BF16)
        O["dbg_q"] = dout("dbg_q", [256, NT_OWN], BF16)
        O["dbg_ssm"] = dout("dbg_ssm", [256, NT_ALL], BF16)
        O["dbg_vc"] = dout("dbg_vc", [128, 15 + NT_OWN + NT_HALO], F32)

    with ExitStack() as es:
        P = Prog(nc, es)
        ones_bf = P.sb("ones_bf", [128, 128], BF16)
        ones_f = P.sb("ones_f", [128, 128], F32)
        ident_f = P.sb("ident_f", [128, 128], F32)
        ident_bf = P.sb("ident_bf", [128, 128], BF16)
        modT = P.sb("modT", [128, 48, 2], F32)
        gsc1 = P.sb("gsc1", [128, 8, 2], F32)
        gsc2 = P.sb("gsc2", [128, 8, 2], F32)
        gvecs = P.sb("gvecs", [128, 8, 4], F32)
        gb1 = P.sb("gb1", [128, 8, 2], F32)
        gb2 = P.sb("gb2", [128, 8, 2], F32)
        mixT = P.sb("mixT", [128, 8, NT_OWN + NCTX], BF16)
        psA = P.ps("psA", [128, 512])
        psB = P.ps("psB", [128, 512])
        psC = P.ps("psC", [128, 512])
        psD = P.ps("psD", [128, 512])
        S_ps = P.ps("S_ps", [128, 1024])
        PT_ps = P.ps("PT_ps", [128, 1024], BF16)
        O_ps = P.ps("O_ps", [128, 512])

        P.op("gpsimd", lambda e: e.memset(ones_bf[:], 1.0), w=["ones_bf"])
        P.op("gpsimd", lambda e: e.memset(ones_f[:], 1.0), w=["ones_f"])
        P.dma("sync", gvecs[:], I["gvecs"], w=["gvecs"])
        I["ident"] = din("ident", [128, 128])
        P.dma("sync", ident_f[:], I["ident"], w=["ident_f"])
        P.op("vector", lambda e: e.tensor_copy(ident_bf[:], ident_f[:]), r=["ident_f"], w=["ident_bf"])

        with ExitStack() as es0:
            P.es = es0
            cT = P.sb("cT", [128, 16], F32)
            sg = P.sb("sg", [128, 16], F32)
            scT = P.sb("scT", [128, 16], F32)
            bm = P.sb("bm", [2, 6 * D], F32)
            modrow = P.sb("modrow", [2, 6 * D], F32)
            wm = [P.sb("wm%d" % i, [128, 8, 512], F32) for i in range(2)]
            P.dma("sync", cT[:], I["cT"], w=["cT"])
            P.dma("sync", bm[:], I["b_mod2"], w=["bm"])
            P.op("scalar", lambda e: e.activation(sg[:], cT[:], AF.Sigmoid), r=["cT"], w=["sg"])
            P.op("vector", lambda e: e.tensor_mul(scT[:], cT[:], sg[:]), r=["cT", "sg"], w=["scT"])
            wmv = I["w_mod"].rearrange("(kc p) n -> p kc n", p=128)
            pss = [psA, psB]
            for nt in range(12):
                w_ = wm[nt % 2]
                wk = "wm%d" % (nt % 2)
                P.dma("sync" if nt % 2 == 0 else "gpsimd", w_[:], wmv[:, :, nt * 512:(nt + 1) * 512], w=[wk])
                ps_ = pss[nt % 2]
                pk = "psA" if nt % 2 == 0 else "psB"
                for kc in range(8):
                    P.op("tensor", (lambda e, ps_=ps_, w_=w_, kc=kc: e.matmul(
                        ps_[0:2, :], scT[:, 2 * kc:2 * kc + 2], w_[:, kc, :], start=(kc == 0), stop=(kc == 7))),
                        r=["scT", wk], w=[pk])
                P.op("vector", (lambda e, ps_=ps_, nt=nt: e.tensor_tensor(
                    modrow[:, nt * 512:(nt + 1) * 512], ps_[0:2, :], bm[:, nt * 512:(nt + 1) * 512], ALU.add)),
                    r=[pk, "bm"], w=["modrow"])
            for j in range(48):
                P.op("tensor", (lambda e, j=j: e.transpose(psC[:, 2 * j:2 * j + 2], modrow[:, j * 128:(j + 1) * 128],
                                                           ident_f[0:2, 0:2])), r=["modrow", "ident_f"], w=["psC"])
            P.op("vector", lambda e: e.tensor_copy(modT[:].rearrange("p a b -> p (a b)"), psC[:, 0:96]),
                 r=["psC"], w=["modT"])
            for (dst, gi, sci, key) in ((gsc1, 0, 1, "gsc1"), (gsc2, 1, 4, "gsc2")):
                for wch in range(2):
                    P.op("vector", (lambda e, dst=dst, gi=gi, sci=sci, wch=wch: e.scalar_tensor_tensor(
                        dst[:, :, wch], modT[:, 8 * sci:8 * sci + 8, wch], 1.0, gvecs[:, :, gi], ALU.add, ALU.mult)),
                        r=["modT", "gvecs"], w=[key])
            for (dst, bi, gti, key) in ((gb1, 2, 2, "gb1"), (gb2, 3, 5, "gb2")):
                for wch in range(2):
                    P.op("vector", (lambda e, dst=dst, bi=bi, gti=gti, wch=wch: e.tensor_mul(
                        dst[:, :, wch], modT[:, 8 * gti:8 * gti + 8, wch], gvecs[:, :, bi])),
                        r=["modT", "gvecs"], w=[key])
            if dbg:
                P.dma("sync", O["dbg_mod"], modT[:].rearrange("p a b -> p (a b)"), r=["modT"], w=["dbg_mod"])
            P.emit()
        P.es = es
        eps_t = P.sb("eps_t", [128, 1], F32)
        P.op("gpsimd", lambda e: e.memset(eps_t[:], EPS), w=["eps_t"])
        esS = ExitStack()
        P.es = esS
        ssmT = P.sb("ssmT", [128, 2, NT_ALL], BF16)
        ssm_c = P.sb("ssm_c", [128, 2, NCTX], BF16)
        esX = ExitStack()
        P.es = esX
        fnetT = P.sb("fnetT", [128, 2, NT_ALL], BF16)
        fnet_c = P.sb("fnet_c", [128, 2, NCTX], BF16)
        qT = P.sb("qT", [128, 2, NT_OWN], BF16)
        q_c = P.sb("q_c", [128, 2, NCTX], BF16)
        kT = P.sb("kT", [128, 2, NT_OWN + NT_HALO], BF16)
        k_c = P.sb("k_c", [128, 2, NCTX], BF16)
        vtok = P.sb("vtok", [128, 18, 4, 65], BF16)
        v_c = P.sb("v_c", [128, 2, 4, 65], BF16)
        P.op("gpsimd", lambda e: e.memset(vtok[:], 1.0), w=["vtok"])
        P.op("gpsimd", lambda e: e.memset(v_c[:], 1.0), w=["v_c"])
        esV = ExitStack()
        P.es = esV
        vconv = P.sb("vconv", [128, 2, 15 + NT_OWN + NT_HALO], F32)
        vconv_c = P.sb("vconv_c", [128, 2, 15 + NCTX + 15], F32)
        P.op("gpsimd", lambda e: e.memset(vconv[:, :, 0:15], 0.0), w=["vconv_pad"])
        P.op("gpsimd", lambda e: e.memset(vconv_c[:], 0.0), w=["vconv_c"])

        def norm_tile(xt, xk, n, which, gsc, shi, h, sq, rstd, tmp, ps_ss, keys=("sq", "h", "tmp", "rstd")):
            ksq, kh, ktmp, krstd = keys
            P.op("scalar", lambda e: e.activation(sq[:, :, 0:n], xt[:, :, 0:n], AF.Square), r=[xk], w=[ksq])
            for c in range(8):
                P.op("tensor", (lambda e, c=c: e.matmul(ps_ss[0][:, 0:n], ones_bf[:], sq[:, c, 0:n],
                                                         start=(c == 0), stop=(c == 7))),
                     r=["ones_bf", ksq], w=[ps_ss[1]])
            P.op("scalar", lambda e: e.activation(tmp[:, 0:n], ps_ss[0][:, 0:n], AF.Sqrt, scale=1.0 / D, bias=eps_t[:, 0:1]),
                 r=[ps_ss[1], "eps_t"], w=[ktmp])
            P.op("vector", lambda e: e.reciprocal(rstd[:, 0:n], tmp[:, 0:n]), r=[ktmp], w=[krstd])
            for c in range(8):
                P.op("vector", (lambda e, c=c: e.tensor_mul(tmp[:, 0:n], xt[:, c, 0:n], rstd[:, 0:n])),
                     r=[xk, krstd], w=[ktmp])
                P.op("scalar", (lambda e, c=c: e.activation(h[:, c, 0:n], tmp[:, 0:n], AF.Identity,
                                                            scale=gsc[:, c, which:which + 1],
                                                            bias=modT[:, shi * 8 + c, which:which + 1])),
                     r=[ktmp, "gsc1", "gsc2", "modT"], w=[kh])

        with ExitStack() as esA:
            P.es = esA
            w_in_bf = P.sb("w_in_bf", [128, 8, 1792], BF16)
            stage = [P.sb("stage%d" % i, [128, 2048], F32) for i in range(2)]
            load_w_bf16(P, w_in_bf, I["w_in"], 8, 1792, stage, "w_in_bf")
            xts = [P.sb("xt%d" % i, [128, 8, TA], F32) for i in range(2)]
            sq = P.sb("sq", [128, 8, TA], BF16)
            h = P.sb("h", [128, 8, TA], BF16)
            tmp = P.sb("tmp", [128, TA], F32)
            rstd = P.sb("rstd", [128, TA], F32)
            sigs = [P.sb("sig%d" % i, [128, TA], F32) for i in range(2)]
            xv = I["xT"].rearrange("(c p) t -> p c t", p=128)
            cv = I["ctxT"].rearrange("(c p) t -> p c t", p=128)
            tiles = [("ctx", 0, NCTX)]
            t0 = 0
            while t0 < NT_ALL:
                tiles.append(("lat", t0, TA))
                t0 += TA
            pcyc = [(psB, "psB"), (psC, "psC"), (psD, "psD")]
            pi = 0
            import os
            tiles = tiles[:int(os.environ.get('DBG_NT', '99'))]
            for ti, (kind, t0, n) in enumerate(tiles):
                xt = xts[ti % 2]
                xk = "xt%d" % (ti % 2)
                P.dma("sync", xt[:, :, 0:n], (cv if kind == "ctx" else xv)[:, :, t0:t0 + n], w=[xk])
                which = 1 if kind == "ctx" else 0
                norm_tile(xt, xk, n, which, gsc1, 0, h, sq, rstd, tmp, (psA, "psA"))
                full = (kind == "ctx") or (t0 < NT_OWN + NT_HALO)
                nfull = n if kind == "ctx" else max(0, min(n, NT_OWN + NT_HALO - t0))
                cols = list(range(14)) if full else [4, 5, 12, 13]
                order = [2, 0, 3, 1] + [j for j in cols if j > 3] if full else cols
                for j in order:
                    if j in (10, 11):
                        continue
                    nn = n if j in (4, 5, 12, 13) else nfull
                    ps_, pk = pcyc[pi % 3]
                    pi += 1
                    for kc in range(8):
                        P.op("tensor", (lambda e, ps_=ps_, j=j, kc=kc, nn=nn: e.matmul(
                            ps_[:, 0:nn], w_in_bf[:, kc, j * 128:(j + 1) * 128], h[:, kc, 0:nn],
                            start=(kc == 0), stop=(kc == 7))), r=["w_in_bf", "h"], w=[pk])
                    if j in (2, 3):
                        P.op("scalar", (lambda e, ps_=ps_, nn=nn, j=j: e.activation(sigs[j - 2][:, 0:nn], ps_[:, 0:nn], AF.Sigmoid)),
                             r=[pk], w=["sig%d" % (j - 2)])
                    elif j in (0, 1):
                        dst = (vconv_c[:, j, 15:15 + nn] if kind == "ctx" else vconv[:, j, 15 + t0:15 + t0 + nn])
                        P.op("vector", (lambda e, ps_=ps_, nn=nn, j=j, dst=dst: e.tensor_mul(dst, ps_[:, 0:nn], sigs[j][:, 0:nn])),
                             r=[pk, "sig%d" % j], w=["vconv_c" if kind == "ctx" else "vconv"])
                    else:
                        if j in (4, 5):
                            dst = fnet_c[:, j - 4, 0:nn] if kind == "ctx" else fnetT[:, j - 4, t0:t0 + nn]; key = "fnet"
                        elif j in (6, 7):
                            nq = nn if kind == "ctx" else max(0, min(nn, NT_OWN - t0))
                            if nq == 0:
                                continue
                            nn = nq
                            dst = q_c[:, j - 6, 0:nn] if kind == "ctx" else qT[:, j - 6, t0:t0 + nn]; key = "q"
                        elif j in (8, 9):
                            dst = k_c[:, j - 8, 0:nn] if kind == "ctx" else kT[:, j - 8, t0:t0 + nn]; key = "k"
                        else:
                            dst = ssm_c[:, j - 12, 0:nn] if kind == "ctx" else ssmT[:, j - 12, t0:t0 + nn]; key = "ssm"
                        eng = "scalar" if (j % 2 == 0) else "vector"
                        if eng == "scalar":
                            P.op("scalar", (lambda e, ps_=ps_, nn=nn, dst=dst: e.activation(dst, ps_[:, 0:nn], AF.Copy)), r=[pk], w=[key])
                        else:
                            P.op("vector", (lambda e, ps_=ps_, nn=nn, dst=dst: e.tensor_copy(dst, ps_[:, 0:nn])), r=[pk], w=[key])
                if full and not os.environ.get('DBG_SKIPV'):
                    for s in range(nfull // 128):
                        ps_, pk = pcyc[pi % 3]
                        pi += 1
                        for kc in range(8):
                            P.op("tensor", (lambda e, ps_=ps_, s=s, kc=kc: e.matmul(
                                ps_[:, 0:256], h[:, kc, s * 128:(s + 1) * 128], w_in_bf[:, kc, 1280:1536],
                                start=(kc == 0), stop=(kc == 7))), r=["w_in_bf", "h"], w=[pk])
                        if kind == "ctx":
                            dst = v_c[:, s, :, 0:64]
                        else:
                            dst = vtok[:, t0 // 128 + s, :, 0:64]
                        P.op("vector", (lambda e, ps_=ps_, dst=dst: e.tensor_copy(
                            dst, ps_[:, 0:256].rearrange("p (h d) -> p h d", h=4))), r=[pk], w=["v"])
            if dbg:
                for cc in range(2):
                    P.dma("sync", O["dbg_q"][cc * 128:(cc + 1) * 128, :], qT[:, cc, :], r=["q"], w=["dbgq%d" % cc])
                    P.dma("sync", O["dbg_ssm"][cc * 128:(cc + 1) * 128, :], ssmT[:, cc, :], r=["ssm"], w=["dbgs%d" % cc])
                P.dma("sync", O["dbg_vc"], vconv[:, 0, :], r=["vconv", "vconv_pad"], w=["dbgvc"])
            P.emit()
        P.es = esX

        import os
        SKIP = os.environ.get("DBG_SKIP", "")
        if "conv" not in SKIP:
          with ExitStack() as esM:
            P.es = esM
            cw = P.sb("cw", [128, 2, 31], F32)
            cvec = P.sb("cvec", [128, 2, 4], F32)
            wpw = P.sb("wpw", [128, 2, 256], BF16)
            stg = [P.sb("cstage%d" % i, [128, 2048], F32) for i in range(2)]
            cvo = P.sb("cvo", [128, 2, NT_OWN], F32)
            zs = P.sb("zs", [128, 2, NT_OWN], BF16)
            csq = P.sb("csq", [128, 2, 512], F32)
            cmean = P.sb("cmean", [128, 512], F32)
            cmsq = P.sb("cmsq", [128, 512], F32)
            cvar = P.sb("cvar", [128, 512], F32)
            crstd = P.sb("crstd", [128, 512], F32)
            cxm = P.sb("cxm", [128, 512], F32)
            P.dma("sync", cw[:], I["conv_wT"], w=["cw"])
            P.dma("sync", cvec[:], I["conv_vec"], w=["cvec"])
            load_w_bf16(P, wpw, I["conv_w_pw"], 2, 256, stg, "wpw")
            jobs = [(vconv, "vconv", NT_OWN, 0)]
            if ctx_out:
                jobs.append((vconv_c, "vconv_c", NCTX, NT_OWN))
            for (src_, skey, n, moff) in jobs:
                for cc in range(2):
                    P.op("vector", (lambda e, cc=cc, n=n, src_=src_: e.tensor_scalar(
                        cvo[:, cc, 0:n], src_[:, cc, 0:n], cw[:, cc, 0:1], None, ALU.mult)),
                        r=[skey, "vconv_pad", "cw"], w=["cvo%d" % cc])
                    for k in range(1, 31):
                        P.op("vector", (lambda e, cc=cc, n=n, k=k, src_=src_: e.scalar_tensor_tensor(
                            cvo[:, cc, 0:n], src_[:, cc, k:k + n], cw[:, cc, k:k + 1], cvo[:, cc, 0:n], ALU.mult, ALU.add)),
                            r=[skey, "vconv_pad", "cw", "cvo%d" % cc], w=["cvo%d" % cc])
                    P.op("gpsimd", (lambda e, cc=cc, n=n: e.tensor_scalar(
                        cvo[:, cc, 0:n], cvo[:, cc, 0:n], cvec[:, cc, 0:1], None, ALU.add)),
                        r=["cvec", "cvo%d" % cc], w=["cvo%d" % cc])
                t0 = 0
                while t0 < n:
                    nn = min(512, n - t0)
                    P.op("scalar", (lambda e, t0=t0, nn=nn: e.activation(csq[:, :, 0:nn], cvo[:, :, t0:t0 + nn], AF.Square)),
                         r=["cvo0", "cvo1"], w=["csq"])
                    for cc in range(2):
                        P.op("tensor", (lambda e, cc=cc, t0=t0, nn=nn: e.matmul(psA[:, 0:nn], ones_f[:], cvo[:, cc, t0:t0 + nn],
                                                                             start=(cc == 0), stop=(cc == 1))),
                             r=["ones_f", "cvo0", "cvo1"], w=["psA"])
                    for cc in range(2):
                        P.op("tensor", (lambda e, cc=cc, nn=nn: e.matmul(psB[:, 0:nn], ones_f[:], csq[:, cc, 0:nn],
                                                                      start=(cc == 0), stop=(cc == 1))),
                             r=["ones_f", "csq"], w=["psB"])
                    P.op("scalar", (lambda e, nn=nn: e.activation(cmean[:, 0:nn], psA[:, 0:nn], AF.Copy, scale=1.0 / 256)),
                         r=["psA"], w=["cmean"])
                    P.op("vector", (lambda e, nn=nn: e.tensor_mul(cmsq[:, 0:nn], cmean[:, 0:nn], cmean[:, 0:nn])),
                         r=["cmean"], w=["cmsq"])
                    P.op("vector", (lambda e, nn=nn: e.scalar_tensor_tensor(cvar[:, 0:nn], psB[:, 0:nn], 1.0 / 256, cmsq[:, 0:nn],
                                                                         ALU.mult, ALU.subtract)),
                         r=["psB", "cmsq"], w=["cvar"])
                    P.op("scalar", (lambda e, nn=nn: e.activation(cvar[:, 0:nn], cvar[:, 0:nn], AF.Sqrt, bias=eps_t[:, 0:1])),
                         r=["cvar", "eps_t"], w=["cvar"])
                    P.op("vector", (lambda e, nn=nn: e.reciprocal(crstd[:, 0:nn], cvar[:, 0:nn])), r=["cvar"], w=["crstd"])
                    for cc in range(2):
                        P.op("vector", (lambda e, cc=cc, t0=t0, nn=nn: e.tensor_sub(cxm[:, 0:nn], cvo[:, cc, t0:t0 + nn], cmean[:, 0:nn])),
                             r=["cvo%d" % cc, "cmean"], w=["cxm"])
                        P.op("vector", (lambda e, nn=nn: e.tensor_mul(cxm[:, 0:nn], cxm[:, 0:nn], crstd[:, 0:nn])),
                             r=["cxm", "crstd"], w=["cxm"])
                        P.op("scalar", (lambda e, cc=cc, t0=t0, nn=nn: e.activation(
                            zs[:, cc, t0:t0 + nn], cxm[:, 0:nn], AF.Silu, scale=cvec[:, cc, 1:2], bias=cvec[:, cc, 2:3])),
                            r=["cxm", "cvec"], w=["zs"])
                    for oc in range(2):
                        ps_, pk = (psC, "psC") if oc == 0 else (psD, "psD")
                        for cc in range(2):
                            P.op("tensor", (lambda e, ps_=ps_, oc=oc, cc=cc, t0=t0, nn=nn: e.matmul(
                                ps_[:, 0:nn], wpw[:, cc, oc * 128:(oc + 1) * 128], zs[:, cc, t0:t0 + nn],
                                start=(cc == 0), stop=(cc == 1))), r=["wpw", "zs"], w=[pk])
                        P.op("scalar", (lambda e, ps_=ps_, oc=oc, t0=t0, nn=nn, moff=moff: e.activation(
                            mixT[:, oc, moff + t0:moff + t0 + nn], ps_[:, 0:nn], AF.Identity, bias=cvec[:, oc, 3:4])),
                            r=[pk, "cvec"], w=["mix_conv"])
                    t0 += nn
            P.emit()

        esV.close()
        P.es = esX
        if "fnet" not in SKIP:
          with ExitStack() as esM:
            P.es = esM
            CSb = P.sb("CSb", [128, 2, 512], BF16)
            fw = P.sb("fw", [128, 2, 256], BF16)
            fb = P.sb("fb", [128, 2], F32)
            stg = [P.sb("fstage%d" % i, [128, 2048], F32) for i in range(2)]
            PQ = P.sb("PQ", [128, 32, 512], BF16)
            tbl = [P.sb("dtbl%d" % i, [128, 4, 2, 512], BF16) for i in range(2)]
            yf = P.sb("yf", [128, 2, NT_OWN], BF16)
            P.dma("sync", CSb[:], I["fnet_cs"], w=["CSb"])
            P.dma("sync", fb[:], I["fnet_b"], w=["fb"])
            load_w_bf16(P, fw, I["fnet_w"], 2, 256, stg, "fw")
            fjobs = [("lat", 32, fnetT, 4, 512, 0)]
            if ctx_out:
                fjobs.append(("ctx", 2, fnet_c, 1, 256, NT_OWN))
            for (kind, ntt, srcT, nkt, kw, moff) in fjobs:
                pc2 = [(psC, "psC"), (psD, "psD")]
                for tt in range(ntt):
                    ps_, pk = pc2[tt % 2]
                    for cc in range(2):
                        P.op("tensor", (lambda e, ps_=ps_, cc=cc, tt=tt, srcT=srcT: e.matmul(
                            ps_[:, :], srcT[:, cc, tt * 128:(tt + 1) * 128], CSb[:, cc, :], start=(cc == 0), stop=(cc == 1))),
                            r=["fnet", "CSb"], w=[pk])
                    if tt % 2 == 0:
                        P.op("scalar", (lambda e, ps_=ps_, tt=tt: e.activation(PQ[:, tt, :], ps_[:, :], AF.Copy)), r=[pk], w=["PQ"])
                    else:
                        P.op("vector", (lambda e, ps_=ps_, tt=tt: e.tensor_copy(PQ[:, tt, :], ps_[:, :])), r=[pk], w=["PQ"])
                nblk = ntt // 4 if ntt >= 4 else 1
                bi = 0
                for kt in range(nkt):
                    for q in range(nblk):
                        tb = tbl[bi % 2]
                        tk = "dtbl%d" % (bi % 2)
                        nt4 = min(4, ntt)
                        if kind == "lat":
                            P.dma("sync" if bi % 2 == 0 else "gpsimd", tb[:], I["dft"][kt, q], w=[tk])
                        else:
                            P.dma("sync", tb[:, 0:2, :, 0:256], I["dftc"], w=[tk])
                        bi += 1
                        for t4 in range(nt4):
                            tt = q * 4 + t4
                            for oc in range(2):
                                ps_, pk = (psA, "psA") if oc == 0 else (psB, "psB")
                                first = (q == 0 and t4 == 0)
                                last = (q == nblk - 1 and t4 == nt4 - 1)
                                P.op("tensor", (lambda e, ps_=ps_, tb=tb, tt=tt, t4=t4, oc=oc, first=first, kw=kw: e.matmul(
                                    ps_[:, 0:kw], PQ[:, tt, oc * 128:(oc + 1) * 128], tb[:, t4, 0, 0:kw], start=first, stop=False)),
                                    r=["PQ", tk], w=[pk])
                                P.op("tensor", (lambda e, ps_=ps_, tb=tb, tt=tt, t4=t4, oc=oc, last=last, kw=kw: e.matmul(
                                    ps_[:, 0:kw], PQ[:, tt, 256 + oc * 128:256 + (oc + 1) * 128], tb[:, t4, 1, 0:kw], start=False, stop=last)),
                                    r=["PQ", tk], w=[pk])
                    P.op("scalar", (lambda e, kt=kt, kw=kw: e.activation(yf[:, 0, kt * 512:kt * 512 + kw], psA[:, 0:kw], AF.Copy)),
                         r=["psA"], w=["yf"])
                    P.op("vector", (lambda e, kt=kt, kw=kw: e.tensor_copy(yf[:, 1, kt * 512:kt * 512 + kw], psB[:, 0:kw])),
                         r=["psB"], w=["yf"])
                    for oc in range(2):
                        ps_, pk = pc2[oc]
                        for cc in range(2):
                            P.op("tensor", (lambda e, ps_=ps_, oc=oc, cc=cc, kt=kt, kw=kw: e.matmul(
                                ps_[:, 0:kw], fw[:, cc, oc * 128:(oc + 1) * 128], yf[:, cc, kt * 512:kt * 512 + kw],
                                start=(cc == 0), stop=(cc == 1))), r=["fw", "yf"], w=[pk])
                        P.op("scalar", (lambda e, ps_=ps_, oc=oc, kt=kt, kw=kw, moff=moff: e.activation(
                            mixT[:, 2 + oc, moff + kt * 512:moff + kt * 512 + kw], ps_[:, 0:kw], AF.Identity, bias=fb[:, oc:oc + 1])),
                            r=[pk, "fb"], w=["mix_fnet"])
            P.emit()
        P.es = esX

        if "na" not in SKIP:
          with ExitStack() as esM:
            P.es = esM
            tabs = P.sb("natab", [128, 3, 4, 640], F32)
            S_sb = P.sb("S_sb", [128, 896], F32)
            Pm = P.sb("Pm", [128, 896], BF16)
            PT_sb = P.sb("PT_sb", [128, 896], BF16)
            otok = P.sb("otok", [128, 256], BF16)
            negmx = P.sb("negmx", [128, 1], F32)
            rinv = P.sb("rinv", [128, 1], F32)
            for v in range(3):
                P.dma("sync" if v % 2 == 0 else "gpsimd", tabs[:, v, :, :], I["natab"][v], w=["natab"])

            def attend(q_ap, kparts, vparts, tab_ap, nk, out_slices):
                for (k_ap, c0, n) in kparts:
                    P.op("tensor", (lambda e, k_ap=k_ap, c0=c0, n=n: e.matmul(S_ps[:, c0:c0 + n], q_ap, k_ap, start=True, stop=True)),
                         r=["q", "k"], w=["S_ps"])
                nloc = 640 if tab_ap is not None else 0
                if tab_ap is not None:
                    P.op("vector", lambda e: e.scalar_tensor_tensor(S_sb[:, 0:640], S_ps[:, 0:640], 0.125, tab_ap, ALU.mult, ALU.add),
                         r=["S_ps", "natab"], w=["S_sb"])
                P.op("scalar", lambda e: e.activation(S_sb[:, nloc:nk], S_ps[:, nloc:nk], AF.Copy, scale=0.125), r=["S_ps"], w=["S_sb"])
                P.op("vector", lambda e: e.tensor_reduce(negmx[:], S_sb[:, 0:nk], AX.X, ALU.max, negate=True), r=["S_sb"], w=["negmx"])
                P.op("scalar", lambda e: e.activation(Pm[:, 0:nk], S_sb[:, 0:nk], AF.Exp, bias=negmx[:, 0:1]), r=["S_sb", "negmx"], w=["Pm"])
                nb = nk // 128
                for blk in range(nb):
                    P.op("tensor", (lambda e, blk=blk: e.transpose(PT_ps[:, blk * 128:(blk + 1) * 128], Pm[:, blk * 128:(blk + 1) * 128], ident_bf[:])),
                         r=["Pm", "ident_bf"], w=["PT_ps"])
                P.op("vector", lambda e: e.tensor_copy(PT_sb[:, 0:nk], PT_ps[:, 0:nk]), r=["PT_ps"], w=["PT_sb"])
                for blk in range(nb):
                    P.op("tensor", (lambda e, blk=blk: e.matmul(O_ps[:, 0:65], PT_sb[:, blk * 128:(blk + 1) * 128], vparts[blk],
                                                              start=(blk == 0), stop=(blk == nb - 1))),
                         r=["PT_sb", "v", "vtok", "v_c"], w=["O_ps"])
                P.op("vector", lambda e: e.reciprocal(rinv[:], O_ps[:, 64:65]), r=["O_ps"], w=["rinv"])
                P.op("scalar", lambda e: e.activation(out_slices, O_ps[:, 0:64], AF.Copy, scale=rinv[:, 0:1]), r=["O_ps", "rinv"], w=["otok"])

            def flush_otok(col0):
                for cc in range(2):
                    P.op("tensor", (lambda e, cc=cc: e.transpose(PT_ps[:, cc * 128:(cc + 1) * 128], otok[:, cc * 128:(cc + 1) * 128], ident_bf[:])),
                         r=["otok", "ident_bf"], w=["PT_ps"])
                P.op("vector", lambda e: e.tensor_copy(mixT[:, 4:6, col0:col0 + 128], PT_ps[:, 0:256].rearrange("p (c t) -> p c t", c=2)),
                     r=["PT_ps"], w=["mix_na"])

            for rp in range(int(os.environ.get("DBG_NRP", "16"))):
                base = min(max(2 * rp - 4, 0), 54)
                v = min(rp, 2)
                for hh in range(4):
                    hc, pb = hh // 2, (hh % 2) * 64
                    q_ap = qT[pb:pb + 64, hc, rp * 128:(rp + 1) * 128]
                    kparts = [(kT[pb:pb + 64, hc, base * 64:base * 64 + 512], 0, 512),
                              (kT[pb:pb + 64, hc, base * 64 + 512:base * 64 + 640], 512, 128),
                              (k_c[pb:pb + 64, hc, 0:256], 640, 256)]
                    vparts = [vtok[:, base // 2 + blk, hh, :] for blk in range(5)] + [v_c[:, blk, hh, :] for blk in range(2)]
                    attend(q_ap, kparts, vparts, tabs[:, v, hh, :], 896, otok[:, hh * 64:(hh + 1) * 64])
                flush_otok(rp * 128)
            if ctx_out:
                for qt in range(2):
                    for hh in range(4):
                        hc, pb = hh // 2, (hh % 2) * 64
                        q_ap = q_c[pb:pb + 64, hc, qt * 128:(qt + 1) * 128]
                        kparts = [(k_c[pb:pb + 64, hc, 0:256], 0, 256)]
                        vparts = [v_c[:, blk, hh, :] for blk in range(2)]
                        attend(q_ap, kparts, vparts, None, 256, otok[:, hh * 64:(hh + 1) * 64])
                    flush_otok(NT_OWN + qt * 128)
            P.emit()
        P.es = esX
        if dbg == "mixin":
            with ExitStack() as esD:
                P.es = esD
                dst_ = P.sb("dmix", [128, 512], F32)
                mv = I["mix_dbg"].rearrange("(c p) t -> p c t", p=128)
                for c in range(8):
                    if (c // 2 == 0 and "conv" not in SKIP) or (c // 2 == 1 and "fnet" not in SKIP) or (c // 2 == 2 and "na" not in SKIP) or (c // 2 == 3 and "ssm" not in SKIP):
                        continue
                    for t0 in range(0, NT_OWN + NCTX, 512):
                        nn = min(512, NT_OWN + NCTX - t0)
                        P.dma("sync", dst_[:, 0:nn], mv[:, c, t0:t0 + nn], w=["dmix"])
                        P.op("vector", (lambda e, c=c, t0=t0, nn=nn: e.tensor_copy(mixT[:, c, t0:t0 + nn], dst_[:, 0:nn])),
                             r=["dmix"], w=["mix_dbg%d" % c])
                P.emit()
        esX.close()
        P.es = esS

        if "ssm" not in SKIP:
          with ExitStack() as es5:
            P.es = es5
            NSC = 544
            W_all = P.sb("W_all", [128, 16, 128], BF16)
            BfR = P.sb("BfR", [128, 2, 8, 128], BF16)
            BfI = P.sb("BfI", [128, 2, 8, 128], BF16)
            CfR = P.sb("CfR", [128, 2, 8, 128], BF16)
            CfI = P.sb("CfI", [128, 2, 8, 128], BF16)
            AAr = P.sb("AAr", [128, 16, 10], F32)
            AAi = P.sb("AAi", [128, 16, 10], F32)
            nAAi = P.sb("nAAi", [128, 16, 10], F32)
            U_all = P.sb("U_all", [128, 16, NSC], BF16)
            Hre = P.sb("Hre", [128, 16, 546], BF16)
            Him = P.sb("Him", [128, 16, 546], BF16)
            Yg = P.sb("Yg", [128, 16, 288], BF16)
            ysT = P.sb("ysT", [128, 2, NT_OWN + NCTX], BF16)
            wglu = P.sb("wglu", [128, 2, 512], BF16)
            bglu = P.sb("bglu", [128, 4], F32)
            P.dma("sync", bglu[:], I["ssm_b_glu"], w=["bglu"])
            KP = ["sprm"]

            with ExitStack() as esP:
                P.es = esP
                spa = P.sb("spa", [128, 3, 16], F32)
                spb = P.sb("spb", [128, 2, 16, 16], F32)
                spc = P.sb("spc", [128, 2, 16, 16], F32)
                spd = P.sb("spd", [128, 16], F32)
                mskF = P.sb("mskF", [128, 128], F32)
                mskB = P.sb("mskB", [128, 128], F32)
                P.dma("sync", spa[:], I["sp_a"], w=KP)
                P.dma("sync", spb[:], I["sp_b"], w=KP)
                P.dma("sync", spc[:], I["sp_c"], w=KP)
                P.dma("sync", spd[:], I["sp_d"], w=KP)
                P.dma("sync", mskF[:], I["mskF"], w=KP)
                P.dma("sync", mskB[:], I["mskB"], w=KP)
                sm = {}
                for nm in ["dt", "xr", "xi", "t", "ab", "s1", "c1", "mag", "magi", "ar1", "ai1", "arm", "aim_", "nr", "den", "cr", "ci",
                           "u1", "u2", "u3", "u4"]:
                    sm[nm] = P.sb("sm_" + nm, [128, 16], F32)
                halfpi = P.sb("halfpi", [128, 1], F32)
                PWr = P.sb("PWr", [128, 16, 17], F32)
                PWi = P.sb("PWi", [128, 16, 17], F32)
                PVr = P.sb("PVr", [128, 16, 17], F32)
                PVi = P.sb("PVi", [128, 16, 17], F32)
                bbr = P.sb("bbr", [128, 16, 16], F32)
                bbi = P.sb("bbi", [128, 16, 16], F32)
                sc = [P.sb("s5sc%d" % i, [128, 8, 8, 16], F32) for i in range(2)]
                X1r = P.sb("X1r", [128, 2, 8, 128], BF16)
                X1i = P.sb("X1i", [128, 2, 8, 128], BF16)
                X2r = P.sb("X2r", [128, 2, 8, 128], BF16)
                X2i = P.sb("X2i", [128, 2, 8, 128], BF16)
                BTr = P.sb("BTr", [128, 2, 8, 128], BF16)
                BTi = P.sb("BTi", [128, 2, 8, 128], BF16)
                Wacc = P.sb("Wacc", [128, 128], F32)
                Wtmp = P.sb("Wtmp", [128, 128], F32)

                def V(fn):
                    P.op("vector", fn, r=KP, w=KP)

                def A(fn):
                    P.op("scalar", fn, r=KP, w=KP)
                are, aim, ldt = spa[:, 0, :], spa[:, 1, :], spa[:, 2, :]
                V(lambda e: e.memset(halfpi[:], float(np.pi / 2)))
                A(lambda e: e.activation(sm["dt"][:], ldt, AF.Exp))
                V(lambda e: e.tensor_mul(sm["xr"][:], are, sm["dt"][:]))
                V(lambda e: e.tensor_mul(sm["xi"][:], aim, sm["dt"][:]))
                for _ in range(5):
                    V(lambda e: e.tensor_scalar(sm["t"][:], sm["xi"][:], float(np.pi), float(2 * np.pi), ALU.is_ge, ALU.mult))
                    V(lambda e: e.tensor_sub(sm["xi"][:], sm["xi"][:], sm["t"][:]))
                A(lambda e: e.activation(sm["s1"][:], sm["xi"][:], AF.Sin))
                A(lambda e: e.activation(sm["ab"][:], sm["xi"][:], AF.Abs))
                A(lambda e: e.activation(sm["c1"][:], sm["ab"][:], AF.Sin, scale=-1.0, bias=halfpi[:, 0:1]))
                A(lambda e: e.activation(sm["mag"][:], sm["xr"][:], AF.Exp))
                A(lambda e: e.activation(sm["magi"][:], sm["xr"][:], AF.Exp, scale=-1.0))
                V(lambda e: e.tensor_mul(sm["ar1"][:], sm["mag"][:], sm["c1"][:]))
                V(lambda e: e.tensor_mul(sm["ai1"][:], sm["mag"][:], sm["s1"][:]))
                V(lambda e: e.tensor_mul(sm["arm"][:], sm["magi"][:], sm["c1"][:]))
                V(lambda e: e.scalar_tensor_tensor(sm["aim_"][:], sm["magi"][:], -1.0, sm["s1"][:], ALU.mult, ALU.mult))
                V(lambda e: e.memset(PWr[:, :, 8], 1.0))
                V(lambda e: e.memset(PWi[:, :, 8], 0.0))

                def cstep(dst, srcj, mr, mi):
                    V(lambda e: e.tensor_mul(sm["u1"][:], PWr[:, :, srcj], mr))
                    V(lambda e: e.tensor_mul(sm["u2"][:], PWi[:, :, srcj], mi))
                    V(lambda e: e.tensor_mul(sm["u3"][:], PWr[:, :, srcj], mi))
                    V(lambda e: e.tensor_mul(sm["u4"][:], PWi[:, :, srcj], mr))
                    V(lambda e: e.tensor_sub(PWr[:, :, dst], sm["u1"][:], sm["u2"][:]))
                    V(lambda e: e.tensor_add(PWi[:, :, dst], sm["u3"][:], sm["u4"][:]))
                for tau in range(1, 9):
                    cstep(8 + tau, 8 + tau - 1, sm["ar1"][:], sm["ai1"][:])
                    cstep(8 - tau, 8 - tau + 1, sm["arm"][:], sm["aim_"][:])
                for j in range(17):
                    P.op("gpsimd", (lambda e, j=j: e.tensor_copy(PVr[:, :, j], PWr[:, :, 16 - j])), r=KP, w=KP)
                    P.op("gpsimd", (lambda e, j=j: e.tensor_copy(PVi[:, :, j], PWi[:, :, 16 - j])), r=KP, w=KP)
                V(lambda e: e.tensor_copy(AAr[:, :, 0], PWr[:, :, 16]))
                V(lambda e: e.tensor_copy(AAi[:, :, 0], PWi[:, :, 16]))
                for k in range(1, 10):
                    V(lambda e, k=k: e.tensor_mul(sm["u1"][:], AAr[:, :, k - 1], AAr[:, :, k - 1]))
                    V(lambda e, k=k: e.tensor_mul(sm["u2"][:], AAi[:, :, k - 1], AAi[:, :, k - 1]))
                    V(lambda e, k=k: e.tensor_sub(AAr[:, :, k], sm["u1"][:], sm["u2"][:]))
                    V(lambda e, k=k: e.scalar_tensor_tensor(AAi[:, :, k], AAr[:, :, k - 1], 2.0, AAi[:, :, k - 1], ALU.mult, ALU.mult))
                V(lambda e: e.tensor_scalar(nAAi[:], AAi[:], -1.0, None, ALU.mult))
                V(lambda e: e.tensor_scalar(sm["nr"][:], sm["ar1"][:], -1.0, None, ALU.add))
                V(lambda e: e.tensor_mul(sm["u1"][:], are, are))
                V(lambda e: e.tensor_mul(sm["u2"][:], aim, aim))
                V(lambda e: e.tensor_add(sm["den"][:], sm["u1"][:], sm["u2"][:]))
                V(lambda e: e.reciprocal(sm["den"][:], sm["den"][:]))
                V(lambda e: e.tensor_mul(sm["u1"][:], sm["nr"][:], are))
                V(lambda e: e.tensor_mul(sm["u2"][:], sm["ai1"][:], aim))
                V(lambda e: e.tensor_add(sm["u1"][:], sm["u1"][:], sm["u2"][:]))
                V(lambda e: e.tensor_mul(sm["cr"][:], sm["u1"][:], sm["den"][:]))
                V(lambda e: e.tensor_mul(sm["u1"][:], sm["ai1"][:], are))
                V(lambda e: e.tensor_mul(sm["u2"][:], sm["nr"][:], aim))
                V(lambda e: e.tensor_sub(sm["u1"][:], sm["u1"][:], sm["u2"][:]))
                V(lambda e: e.tensor_mul(sm["ci"][:], sm["u1"][:], sm["den"][:]))
                crb = sm["cr"][:].unsqueeze(2).broadcast_to([128, 16, 16])
                cib = sm["ci"][:].unsqueeze(2).broadcast_to([128, 16, 16])
                s0 = sc[0][:].rearrange("p a b c -> p (a b c)")[:, 0:256].rearrange("p (a b) -> p a b", a=16)
                s1_ = sc[1][:].rearrange("p a b c -> p (a b c)")[:, 0:256].rearrange("p (a b) -> p a b", a=16)
                V(lambda e: e.tensor_mul(s0, spb[:, 0, :, :], crb))
                V(lambda e: e.tensor_mul(s1_, spb[:, 1, :, :], cib))
                V(lambda e: e.tensor_sub(bbr[:], s0, s1_))
                V(lambda e: e.tensor_mul(s0, spb[:, 1, :, :], crb))
                V(lambda e: e.tensor_mul(s1_, spb[:, 0, :, :], cib))
                V(lambda e: e.tensor_add(bbi[:], s0, s1_))

                def cprod(outr, outi, Pr_, Pi_, Br_, Bi_, neg_im):
                    Prb = Pr_.unsqueeze(3).broadcast_to([128, 8, 8, 16])
                    Pib = Pi_.unsqueeze(3).broadcast_to([128, 8, 8, 16])
                    Brb = Br_.unsqueeze(2).broadcast_to([128, 8, 8, 16])
                    Bib = Bi_.unsqueeze(2).broadcast_to([128, 8, 8, 16])
                    o_r = outr.rearrange("p a (b c) -> p a b c", b=8)
                    o_i = outi.rearrange("p a (b c) -> p a b c", b=8)
                    V(lambda e: e.tensor_mul(sc[0][:], Prb, Brb))
                    V(lambda e: e.tensor_mul(sc[1][:], Pib, Bib))
                    V(lambda e: e.tensor_sub(o_r, sc[0][:], sc[1][:]))
                    V(lambda e: e.tensor_mul(sc[0][:], Prb, Bib))
                    V(lambda e: e.tensor_mul(sc[1][:], Pib, Brb))
                    if neg_im:
                        V(lambda e: e.scalar_tensor_tensor(o_i, sc[0][:], -1.0, sc[1][:], ALU.mult, ALU.subtract))
                    else:
                        V(lambda e: e.tensor_add(o_i, sc[0][:], sc[1][:]))
                for d in range(2):
                    ds = slice(d * 8, (d + 1) * 8)
                    if d == 0:
                        Pst = (PVr[:, ds, 1:9], PVi[:, ds, 1:9])
                        X1t = (PVr[:, ds, 8:16], PVi[:, ds, 8:16])
                        Rt = (PWr[:, ds, 9:17], PWi[:, ds, 9:17])
                        X2t = (PWr[:, ds, 8:16], PWi[:, ds, 8:16])
                    else:
                        Pst = (PWr[:, ds, 8:16], PWi[:, ds, 8:16])
                        X1t = Pst
                        Rt = (PVr[:, ds, 0:8], PVi[:, ds, 0:8])
                        X2t = (PVr[:, ds, 8:16], PVi[:, ds, 8:16])
                    cprod(BTr[:, d], BTi[:, d], Pst[0], Pst[1], bbr[:, ds, :], bbi[:, ds, :], False)
                    cprod(X1r[:, d], X1i[:, d], X1t[0], X1t[1], bbr[:, ds, :], bbi[:, ds, :], False)
                    cprod(CfR[:, d], CfI[:, d], Rt[0], Rt[1], spc[:, 0, ds, :], spc[:, 1, ds, :], True)
                    cprod(X2r[:, d], X2i[:, d], X2t[0], X2t[1], spc[:, 0, ds, :], spc[:, 1, ds, :], True)
                for d in range(2):
                    for pr in range(8):
                        for (srcB, dstB) in ((BTr, BfR), (BTi, BfI)):
                            P.op("tensor", (lambda e, srcB=srcB, d=d, pr=pr: e.transpose(PT_ps[:, 0:128], srcB[:, d, pr, :], ident_bf[:])),
                                 r=KP + ["ident_bf"], w=["PT_ps"])
                            P.op("vector", (lambda e, dstB=dstB, d=d, pr=pr: e.tensor_copy(dstB[:, d, pr, :], PT_ps[:, 0:128])),
                                 r=["PT_ps"], w=KP)
                for g in range(16):
                    pr, ee = g // 2, g % 2
                    rows = slice(ee * 64, ee * 64 + 64)
                    for d in range(2):
                        ps_, pk = (psA, "psA") if d == 0 else (psB, "psB")
                        P.op("tensor", (lambda e, ps_=ps_, d=d, pr=pr, rows=rows: e.matmul(
                            ps_[:, 0:128], X1r[rows, d, pr, :], X2r[rows, d, pr, :], start=True, stop=False)), r=KP, w=[pk])
                        P.op("tensor", (lambda e, ps_=ps_, d=d, pr=pr, rows=rows: e.matmul(
                            ps_[:, 0:128], X1i[rows, d, pr, :], X2i[rows, d, pr, :], start=False, stop=True)), r=KP, w=[pk])
                    V(lambda e: e.tensor_mul(Wacc[:], psA[:, 0:128], mskF[:]))
                    V(lambda e: e.tensor_mul(Wtmp[:], psB[:, 0:128], mskB[:]))
                    V(lambda e: e.tensor_add(Wacc[:], Wacc[:], Wtmp[:]))
                    V(lambda e, g=g: e.scalar_tensor_tensor(W_all[:, g, :], ident_f[:], spd[:, g:g + 1], Wacc[:], ALU.mult, ALU.add))
                    P.op("vector", lambda e: e.tensor_copy(Wtmp[0:1, 0:1], psA[0:1, 0:1]), r=["psA", "psB"] + KP, w=KP)
                P.emit()
            P.es = es5

            with ExitStack() as esR:
                P.es = esR
                Sel = P.sb("Sel", [128, 64, 128], BF16)
                P.dma("sync", Sel[:], I["sel"], w=["Sel"])
                sv = ssmT[:, :, :].rearrange("p c (n s) -> p c s n", s=8)
                scv = ssm_c[:, :, :].rearrange("p c (n s) -> p c s n", s=8)
                for g in range(16):
                    cc, gl = g // 8, g % 8
                    for s in range(8):
                        P.op("tensor", (lambda e, cc=cc, gl=gl, s=s: e.matmul(psA[:, 0:512], Sel[:, gl * 8 + s, :], sv[:, cc, s, 0:512],
                                                                             start=(s == 0), stop=(s == 7))), r=["Sel", "ssm"], w=["psA"])
                    for s in range(8):
                        P.op("tensor", (lambda e, cc=cc, gl=gl, s=s: e.matmul(psB[:, 0:32], Sel[:, gl * 8 + s, :], scv[:, cc, s, 0:32],
                                                                             start=(s == 0), stop=(s == 7))), r=["Sel", "ssm"], w=["psB"])
                    P.op("scalar", (lambda e, g=g: e.activation(U_all[:, g, 32:544], psA[:, 0:512], AF.Copy)), r=["psA"], w=["U_all"])
                    P.op("vector", (lambda e, g=g: e.tensor_copy(U_all[:, g, 0:32], psB[:, 0:32])), r=["psB"], w=["U_all"])
                P.emit()
            P.es = es5

            with ExitStack() as esN:
                P.es = esN
                XB = [[P.sb("XB%d%d" % (i, j), [128, 546], F32) for j in range(2)] for i in range(2)]
                for d in range(2):
                    for pr in range(8):
                        idx = d * 8 + pr
                        lo, hi = (1, 545) if d == 0 else (0, 544)
                        for ri, Bf in ((0, BfR), (1, BfI)):
                            for ee in range(2):
                                g = 2 * pr + ee
                                P.op("tensor", (lambda e, Bf=Bf, d=d, pr=pr, ee=ee, g=g: e.matmul(
                                    psA[ee * 64:(ee + 1) * 64, 0:512], Bf[:, d, pr, ee * 64:(ee + 1) * 64], U_all[:, g, 32:544],
                                    start=True, stop=True)), r=KP + ["U_all"], w=["psA"])
                                P.op("tensor", (lambda e, Bf=Bf, d=d, pr=pr, ee=ee, g=g: e.matmul(
                                    psB[ee * 64:(ee + 1) * 64, 0:32], Bf[:, d, pr, ee * 64:(ee + 1) * 64], U_all[:, g, 0:32],
                                    start=True, stop=True)), r=KP + ["U_all"], w=["psB"])
                            X = XB[0][ri]
                            xk = "XB0%d" % ri
                            if d == 0:
                                P.op("gpsimd", (lambda e, X=X: e.memset(X[:, 0:1], 0.0)), w=[xk])
                                P.op("vector", (lambda e, X=X: e.tensor_copy(X[:, 1:33], psB[:, 0:32])), r=["psB"], w=[xk])
                                P.op("scalar", (lambda e, X=X: e.activation(X[:, 33:545], psA[:, 0:512], AF.Copy)), r=["psA"], w=[xk])
                            else:
                                P.op("gpsimd", (lambda e, X=X: e.memset(X[:, 544:546], 0.0)), w=[xk])
                                P.op("vector", (lambda e, X=X: e.tensor_copy(X[:, 512:544], psB[:, 0:32])), r=["psB"], w=[xk])
                                P.op("scalar", (lambda e, X=X: e.activation(X[:, 0:512], psA[:, 0:512], AF.Copy)), r=["psA"], w=[xk])
                        cur = 0
                        for k in range(10):
                            sh = 1 << k
                            Xr, Xi = XB[cur]
                            Yr, Yi = XB[1 - cur]
                            kx = ["XB%d0" % cur, "XB%d1" % cur]
                            ky = ["XB%d0" % (1 - cur), "XB%d1" % (1 - cur)]
                            ar_ = AAr[:, idx, k:k + 1]
                            ai_ = AAi[:, idx, k:k + 1]
                            nai_ = nAAi[:, idx, k:k + 1]
                            if d == 0:
                                dsl, ssl, csl = slice(lo + sh, hi), slice(lo, hi - sh), slice(lo, lo + sh)
                            else:
                                dsl, ssl, csl = slice(lo, hi - sh), slice(lo + sh, hi), slice(hi - sh, hi)
                            P.op("vector", (lambda e, Yr=Yr, Xr=Xr, ar_=ar_, dsl=dsl, ssl=ssl: e.scalar_tensor_tensor(
                                Yr[:, dsl], Xr[:, ssl], ar_, Xr[:, dsl], ALU.mult, ALU.add)), r=kx + KP, w=[ky[0]])
                            P.op("vector", (lambda e, Yr=Yr, Xi=Xi, nai_=nai_, dsl=dsl, ssl=ssl: e.scalar_tensor_tensor(
                                Yr[:, dsl], Xi[:, ssl], nai_, Yr[:, dsl], ALU.mult, ALU.add)), r=kx + [ky[0]] + KP, w=[ky[0]])
                            P.op("vector", (lambda e, Yi=Yi, Xi=Xi, ar_=ar_, dsl=dsl, ssl=ssl: e.scalar_tensor_tensor(
                                Yi[:, dsl], Xi[:, ssl], ar_, Xi[:, dsl], ALU.mult, ALU.add)), r=kx + KP, w=[ky[1]])
                            P.op("vector", (lambda e, Yi=Yi, Xr=Xr, ai_=ai_, dsl=dsl, ssl=ssl: e.scalar_tensor_tensor(
                                Yi[:, dsl], Xr[:, ssl], ai_, Yi[:, dsl], ALU.mult, ALU.add)), r=kx + [ky[1]] + KP, w=[ky[1]])
                            P.op("gpsimd", (lambda e, Yr=Yr, Xr=Xr, csl=csl: e.tensor_copy(Yr[:, csl], Xr[:, csl])), r=[kx[0]], w=[ky[0]])
                            P.op("gpsimd", (lambda e, Yi=Yi, Xi=Xi, csl=csl: e.tensor_copy(Yi[:, csl], Xi[:, csl])), r=[kx[1]], w=[ky[1]])
                            cur = 1 - cur
                        Xr, Xi = XB[cur]
                        P.op("scalar", (lambda e, Xr=Xr, idx=idx: e.activation(Hre[:, idx, :], Xr[:, :], AF.Copy)), r=["XB%d0" % cur], w=["Hst"])
                        P.op("gpsimd", (lambda e, Xi=Xi, idx=idx: e.tensor_copy(Him[:, idx, :], Xi[:, :])), r=["XB%d1" % cur], w=["Hst"])
                P.emit()
            P.es = es5

            with ExitStack() as esO:
                P.es = esO
                SelT = P.sb("SelT", [128, 64, 128], BF16)
                stg = [P.sb("gstage%d" % i, [128, 2048], F32) for i in range(2)]
                sgl = P.sb("sgl", [128, 512], F32)
                P.dma("sync", SelT[:], I["selT"], w=["SelT"])
                load_w_bf16(P, wglu, I["ssm_w_glu"], 2, 512, stg, "wglu")
                ranges = [(256, 32, 32, 1, 0)]
                if ctx_out:
                    ranges.append((32, 0, 0, 513, 256))
                for g in range(16):
                    pr, ee = g // 2, g % 2
                    rows = slice(ee * 64, ee * 64 + 64)
                    for (nc_, u0, f0, b0, y0) in ranges:
                        ps_, pk = (psA, "psA") if nc_ == 256 else (psB, "psB")
                        P.op("tensor", (lambda e, ps_=ps_, g=g, nc_=nc_, u0=u0: e.matmul(
                            ps_[:, 0:nc_], W_all[:, g, :], U_all[:, g, u0:u0 + nc_], start=True, stop=False)), r=KP + ["U_all"], w=[pk])
                        for (Cf, Hs, d, c0, last) in ((CfR, Hre, 0, f0, False), (CfI, Him, 0, f0, False),
                                                       (CfR, Hre, 1, b0, False), (CfI, Him, 1, b0, True)):
                            P.op("tensor", (lambda e, ps_=ps_, Cf=Cf, Hs=Hs, d=d, c0=c0, last=last, pr=pr, rows=rows, nc_=nc_: e.matmul(
                                ps_[:, 0:nc_], Cf[rows, d, pr, :], Hs[rows, d * 8 + pr, c0:c0 + nc_], start=False, stop=last)),
                                r=KP + ["Hst"], w=[pk])
                        if nc_ == 256:
                            P.op("scalar", (lambda e, ps_=ps_, g=g, y0=y0, nc_=nc_: e.activation(Yg[:, g, y0:y0 + nc_], ps_[:, 0:nc_], AF.Copy)),
                                 r=[pk], w=["Yg"])
                        else:
                            P.op("vector", (lambda e, ps_=ps_, g=g, y0=y0, nc_=nc_: e.tensor_copy(Yg[:, g, y0:y0 + nc_], ps_[:, 0:nc_])),
                                 r=[pk], w=["Yg"])
                ttiles = [(t0, 512) for t0 in range(0, NT_OWN, 512)]
                if ctx_out:
                    ttiles.append((NT_OWN, NCTX))
                pcy2 = [(psC, "psC"), (psD, "psD")]
                pi2 = 0
                for cc in range(2):
                    for (t0, n) in ttiles:
                        ps_, pk = pcy2[pi2 % 2]
                        pi2 += 1
                        nsub = n // 8
                        y0 = t0 // 8
                        pv = ps_[:, 0:n].rearrange("p (n t) -> p t n", t=8)
                        for t in range(8):
                            for gl in range(8):
                                P.op("tensor", (lambda e, pv=pv, t=t, gl=gl, cc=cc, y0=y0, nsub=nsub: e.matmul(
                                    pv[:, t, :], SelT[:, gl * 8 + t, :], Yg[:, cc * 8 + gl, y0:y0 + nsub],
                                    start=(gl == 0), stop=(gl == 7))), r=["SelT", "Yg"], w=[pk])
                        P.op("scalar" if pi2 % 2 == 0 else "vector",
                             (lambda e, ps_=ps_, cc=cc, t0=t0, n=n: (e.activation(ysT[:, cc, t0:t0 + n], ps_[:, 0:n], AF.Copy)
                                                                      if hasattr(e, "activation") else e.tensor_copy(ysT[:, cc, t0:t0 + n], ps_[:, 0:n]))),
                             r=[pk], w=["ysT"])
                for (t0, n) in ttiles:
                    for oc in range(2):
                        for cc in range(2):
                            P.op("tensor", (lambda e, oc=oc, cc=cc, t0=t0, n=n: e.matmul(
                                psA[:, 0:n], wglu[:, cc, oc * 128:(oc + 1) * 128], ysT[:, cc, t0:t0 + n], start=(cc == 0), stop=(cc == 1))),
                                r=["wglu", "ysT"], w=["psA"])
                        for cc in range(2):
                            P.op("tensor", (lambda e, oc=oc, cc=cc, t0=t0, n=n: e.matmul(
                                psB[:, 0:n], wglu[:, cc, 256 + oc * 128:256 + (oc + 1) * 128], ysT[:, cc, t0:t0 + n], start=(cc == 0), stop=(cc == 1))),
                                r=["wglu", "ysT"], w=["psB"])
                        P.op("scalar", (lambda e, oc=oc, n=n: e.activation(sgl[:, 0:n], psB[:, 0:n], AF.Sigmoid, bias=bglu[:, 2 + oc:3 + oc])),
                             r=["psB", "bglu"], w=["sgl"])
                        P.op("vector", (lambda e, oc=oc, t0=t0, n=n: e.scalar_tensor_tensor(
                            mixT[:, 6 + oc, t0:t0 + n], psA[:, 0:n], bglu[:, oc:oc + 1], sgl[:, 0:n], ALU.add, ALU.mult)),
                            r=["psA", "bglu", "sgl"], w=["mix_ssm"])
                P.emit()
            P.es = es5
        P.es = esS
        if dbg:
            for c in range(8):
                P.dma("sync", O["dbg_mix"][c * 128:(c + 1) * 128, :], mixT[:, c, :], r=["mix_conv", "mix_fnet", "mix_na", "mix_ssm", "mix_dbg%d" % c], w=["dbgmix%d" % c])
            P.emit()
        esS.close()
        P.es = es
        with ExitStack() as esC:
            P.es = esC
            w_out_bf = P.sb("w_out_bf", [128, 8, 1024], BF16)
            stage = [P.sb("stageC%d" % i, [128, 2048], F32) for i in range(2)]
            load_w_bf16(P, w_out_bf, I["w_out"], 8, 1024, stage, "w_out_bf")
            ggrp = P.sb("ggrp", [128, 8], F32)
            gfin = P.sb("gfin", [128, 8], F32)
            b1 = P.sb("b1", [128, 32], F32)
            P.dma("sync", ggrp[:], I["g_group"], w=["ggrp"])
            P.dma("sync", gfin[:], I["g_final"], w=["gfin"])
            P.dma("sync", b1[:], I["b_ff1"], w=["b1"])
            xin = P.sb("xin", [128, 8, 512], F32)
            xmid = P.sb("xmid", [128, 8, 512], F32)
            xo = xin
            mn = P.sb("mn", [128, 8, 512], BF16)
            sq = P.sb("sqC", [128, 8, 512], BF16)
            h = P.sb("hC", [128, 8, 512], BF16)
            tmp = P.sb("tmpC", [128, 512], F32)
            rstd = P.sb("rstdC", [128, 512], F32)
            hact = P.sb("hact", [128, 32, 512], BF16)
            w1b = [P.sb("w1b%d" % i, [128, 8, 512], BF16) for i in range(2)]
            w2b = [P.sb("w2b%d" % i, [128, 32, 128], BF16) for i in range(2)]
            xv = I["xT"].rearrange("(c p) t -> p c t", p=128)
            cv = I["ctxT"].rearrange("(c p) t -> p c t", p=128)
            xov = O["xoT"].rearrange("(c p) t -> p c t", p=128)
            tilesC = [("lat", t0, 512) for t0 in range(0, NT_OWN, 512)]
            if ctx_out:
                tilesC.append(("ctx", 0, NCTX))
                cov = O["ctxoT"].rearrange("(c p) t -> p c t", p=128)
            tilesC = tilesC[:int(os.environ.get("DBG_NTC", "99"))]
            w2v = I["w_ff2"].rearrange("(kc p) n -> p kc n", p=128)
            for (kind, t0, n) in tilesC:
                which = 1 if kind == "ctx" else 0
                moff = NT_OWN if kind == "ctx" else t0
                P.dma("sync", xin[:, :, 0:n], (cv if kind == "ctx" else xv)[:, :, t0:t0 + n], r=["xo"], w=["xin", "xo"])
                mixkeys = ["mix_conv", "mix_fnet", "mix_na", "mix_ssm"] + ["mix_dbg%d" % c for c in range(8)]
                P.op("scalar", (lambda e, n=n, moff=moff: e.activation(sq[:, :, 0:n], mixT[:, :, moff:moff + n], AF.Square)),
                     r=mixkeys, w=["sqC"])
                for g4 in range(4):
                    for cc in range(2):
                        P.op("tensor", (lambda e, g4=g4, cc=cc, n=n: e.matmul(psA[:, 0:n], ones_bf[:], sq[:, 2 * g4 + cc, 0:n],
                                                                          start=(cc == 0), stop=(cc == 1))),
                             r=["ones_bf", "sqC"], w=["psA"])
                    P.op("scalar", (lambda e, n=n: e.activation(tmp[:, 0:n], psA[:, 0:n], AF.Sqrt, scale=1.0 / 256, bias=eps_t[:, 0:1])),
                         r=["psA", "eps_t"], w=["tmpC"])
                    P.op("vector", (lambda e, n=n: e.reciprocal(rstd[:, 0:n], tmp[:, 0:n])), r=["tmpC"], w=["rstdC"])
                    for cc in range(2):
                        c = 2 * g4 + cc
                        P.op("vector", (lambda e, c=c, n=n, moff=moff: e.scalar_tensor_tensor(
                            mn[:, c, 0:n], mixT[:, c, moff:moff + n], ggrp[:, c:c + 1], rstd[:, 0:n], ALU.mult, ALU.mult)),
                            r=mixkeys + ["ggrp", "rstdC"], w=["mn"])
                pcy = [(psB, "psB"), (psC, "psC"), (psD, "psD")]
                pi = 0
                for oc in range(8):
                    ps_, pk = pcy[pi % 3]
                    pi += 1
                    for kc in range(8):
                        P.op("tensor", (lambda e, ps_=ps_, oc=oc, kc=kc, n=n: e.matmul(
                            ps_[:, 0:n], w_out_bf[:, kc, oc * 128:(oc + 1) * 128], mn[:, kc, 0:n],
                            start=(kc == 0), stop=(kc == 7))), r=["w_out_bf", "mn"], w=[pk])
                    P.op("scalar", (lambda e, ps_=ps_, oc=oc, n=n, which=which: e.activation(
                        tmp[:, 0:n], ps_[:, 0:n], AF.Identity, scale=modT[:, 16 + oc, which:which + 1],
                        bias=gb1[:, oc, which:which + 1])), r=[pk, "modT", "gb1"], w=["tmpC"])
                    P.op("vector", (lambda e, oc=oc, n=n: e.tensor_add(xmid[:, oc, 0:n], tmp[:, 0:n], xin[:, oc, 0:n])),
                         r=["tmpC", "xin"], w=["xmid"])
                norm_tile(xmid, "xmid", n, which, gsc2, 3, h, sq, rstd, tmp, (psA, "psA"), keys=("sqC", "h", "tmpC", "rstdC"))
                for hb in range(8):
                    wb = w1b[hb % 2]
                    wk = "w1b%d" % (hb % 2)
                    load_w_bf16(P, wb, I["w_ff1"], 8, 512, stage, wk, col0=hb * 512)
                    for m4 in range(4):
                        hidx = hb * 4 + m4
                        ps_, pk = pcy[pi % 3]
                        pi += 1
                        for kc in range(8):
                            P.op("tensor", (lambda e, ps_=ps_, wb=wb, m4=m4, kc=kc, n=n: e.matmul(
                                ps_[:, 0:n], wb[:, kc, m4 * 128:(m4 + 1) * 128], h[:, kc, 0:n],
                                start=(kc == 0), stop=(kc == 7))), r=[wk, "h"], w=[pk])
                        P.op("scalar", (lambda e, ps_=ps_, hidx=hidx, n=n: e.activation(
                            tmp[:, 0:n], ps_[:, 0:n], AF.Relu, bias=b1[:, hidx:hidx + 1])), r=[pk, "b1"], w=["tmpC"])
                        P.op("vector", (lambda e, hidx=hidx, n=n: e.tensor_mul(hact[:, hidx, 0:n], tmp[:, 0:n], tmp[:, 0:n])),
                             r=["tmpC"], w=["hact"])
                for oc in range(8):
                    wb = w2b[oc % 2]
                    wk = "w2b%d" % (oc % 2)
                    for half in range(2):
                        st = stage[half][:, :].rearrange("p (a b) -> p a b", a=16)
                        sk = "stgbuf%d" % id(stage[half])
                        P.dma("sync" if half == 0 else "gpsimd", st, w2v[:, half * 16:(half + 1) * 16, oc * 128:(oc + 1) * 128], w=[sk])
                        P.op("gpsimd" if half == 0 else "vector", (lambda e, wb=wb, st=st, half=half: e.tensor_copy(
                            wb[:, half * 16:(half + 1) * 16, :], st)), r=[sk], w=[wk])
                    ps_, pk = pcy[pi % 3]
                    pi += 1
                    for kc in range(32):
                        P.op("tensor", (lambda e, ps_=ps_, wb=wb, kc=kc, n=n: e.matmul(
                            ps_[:, 0:n], wb[:, kc, :], hact[:, kc, 0:n], start=(kc == 0), stop=(kc == 31))),
                            r=[wk, "hact"], w=[pk])
                    P.op("scalar", (lambda e, ps_=ps_, oc=oc, n=n, which=which: e.activation(
                        tmp[:, 0:n], ps_[:, 0:n], AF.Identity, scale=modT[:, 40 + oc, which:which + 1],
                        bias=gb2[:, oc, which:which + 1])), r=[pk, "modT", "gb2"], w=["tmpC"])
                    P.op("vector", (lambda e, oc=oc, n=n: e.tensor_add(xo[:, oc, 0:n], tmp[:, 0:n], xmid[:, oc, 0:n])),
                         r=["tmpC", "xmid"], w=["xo", "xin"])
                if final_norm and kind == "lat":
                    P.op("scalar", (lambda e, n=n: e.activation(sq[:, :, 0:n], xo[:, :, 0:n], AF.Square)), r=["xo"], w=["sqC"])
                    for c in range(8):
                        P.op("tensor", (lambda e, c=c, n=n: e.matmul(psA[:, 0:n], ones_bf[:], sq[:, c, 0:n],
                                                                  start=(c == 0), stop=(c == 7))), r=["ones_bf", "sqC"], w=["psA"])
                    P.op("scalar", (lambda e, n=n: e.activation(tmp[:, 0:n], psA[:, 0:n], AF.Sqrt, scale=1.0 / D, bias=eps_t[:, 0:1])),
                         r=["psA", "eps_t"], w=["tmpC"])
                    P.op("vector", (lambda e, n=n: e.reciprocal(rstd[:, 0:n], tmp[:, 0:n])), r=["tmpC"], w=["rstdC"])
                    for c in range(8):
                        P.op("vector", (lambda e, c=c, n=n: e.scalar_tensor_tensor(
                            xo[:, c, 0:n], xo[:, c, 0:n], gfin[:, c:c + 1], rstd[:, 0:n], ALU.mult, ALU.mult)),
                            r=["xo", "gfin", "rstdC"], w=["xo"])
                dstv = cov if kind == "ctx" else xov
                P.dma("sync", dstv[:, :, t0:t0 + n], xo[:, :, 0:n], r=["xo"], w=["out_%s_%d" % (kind, t0)])
            P.emit()
        P.es = es
    return nc


def fm(v, nchunk):
    return np.ascontiguousarray(np.asarray(v, np.float32).reshape(nchunk, 128).T)


_FNET_CACHE = {}


def ssm_layout(inp, l, hf):
    import ml_dtypes
    f32 = np.float32
    dd = [1, 0] if hf == 1 else [0, 1]

    def gp(a):
        a = a[dd]
        sh = a.shape
        a = a.reshape((2, 8, 2, 64) + sh[3:])
        perm = (2, 3, 0, 1) + tuple(range(4, a.ndim))
        a = a.transpose(perm)
        return a.reshape((128, 2, 8) + sh[3:])
    are = gp(inp["ssm_a_re"][l])
    aim = gp(inp["ssm_a_im"][l])
    ldt = gp(np.repeat(inp["ssm_log_dt"][l][:, :, None], 64, axis=2))
    out = {"sp_a": np.ascontiguousarray(np.stack([are, aim, ldt], 1).reshape(128, 3, 16).astype(f32))}
    br, bi = gp(inp["ssm_b_re"][l]), gp(inp["ssm_b_im"][l])
    out["sp_b"] = np.ascontiguousarray(np.stack([br, bi], 1).reshape(128, 2, 16, 16).astype(f32))
    cr = gp(inp["ssm_c_re"][l].transpose(0, 1, 3, 2))
    ci = gp(inp["ssm_c_im"][l].transpose(0, 1, 3, 2))
    out["sp_c"] = np.ascontiguousarray(np.stack([cr, ci], 1).reshape(128, 2, 16, 16).astype(f32))
    dsk = inp["ssm_d"][l].reshape(16, 16)
    out["sp_d"] = np.ascontiguousarray(np.tile(dsk.T, (8, 1)).astype(f32))
    s_idx = np.arange(128) // 16
    out["mskF"] = (s_idx[None, :] >= s_idx[:, None]).astype(f32)
    out["mskB"] = (s_idx[None, :] <= s_idx[:, None]).astype(f32)
    sel = np.zeros((128, 8, 8, 128), f32)
    selT = np.zeros((128, 8, 8, 128), f32)
    for gl in range(8):
        for s in range(8):
            for mm in range(16):
                sel[16 * gl + mm, gl, s, 16 * s + mm] = 1.0
                selT[16 * s + mm, gl, s, 16 * gl + mm] = 1.0
    out["sel"] = sel.reshape(128, 64, 128).astype(ml_dtypes.bfloat16)
    out["selT"] = selT.reshape(128, 64, 128).astype(ml_dtypes.bfloat16)
    out["ssm_w_glu"] = np.ascontiguousarray(inp["ssm_w_glu"][l])
    out["ssm_b_glu"] = fm(inp["ssm_b_glu"][l], 4)
    return out


def na_table(rpb, hf):
    out = np.full((3, 128, 4, 640), -1e30, np.float32)
    for v in range(3):
        rp = v
        base = min(max(2 * rp - 4, 0), 54)
        q = np.arange(128)
        qr_p, qc_p = 2 * rp + q // 64, q % 64
        key = np.arange(640)
        kr_p, kc_p = base + key // 64, key % 64
        tr = (lambda a: 63 - a) if hf == 1 else (lambda a: a)
        qr, qc, kr, kc = tr(qr_p), tr(qc_p), tr(kr_p), tr(kc_p)
        rs = np.clip(qr - 4, 0, 56)
        cs = np.clip(qc - 8, 0, 48)
        valid = ((kr[None, :] >= rs[:, None]) & (kr[None, :] < rs[:, None] + 8) &
                 (kc[None, :] >= cs[:, None]) & (kc[None, :] < cs[:, None] + 16))
        dr = np.clip(kr[None, :] - qr[:, None] + 7, 0, 14)
        dc = np.clip(kc[None, :] - qc[:, None] + 15, 0, 30)
        for h in range(4):
            out[v, :, h, :] = np.where(valid, rpb[h][dr, dc], np.float32(-1e30))
    return out


def fnet_consts(hf):
    import ml_dtypes
    if hf in _FNET_CACHE:
        return _FNET_CACHE[hf]
    bf = ml_dtypes.bfloat16
    ch = np.arange(256)
    same = (ch[:, None] // 64) == (ch[None, :] // 64)
    ang = 2 * np.pi * ((ch[:, None] % 64) * (ch[None, :] % 64) % 64) / 64.0
    Cbd = np.where(same, np.cos(ang), 0.0)
    Sbd = np.where(same, np.sin(ang), 0.0)
    cs = np.concatenate([Cbd, Sbd], 1)
    out = {"fnet_cs": np.ascontiguousarray(cs.reshape(2, 128, 512).transpose(1, 0, 2)).astype(bf)}

    def tables(L, nk):
        n = np.arange(L)
        k = np.arange(nk)
        if hf == 1:
            n = L - 1 - n
            k = L - 1 - k
        prod = (n[:, None].astype(np.int64) * k[None, :].astype(np.int64)) % L
        a = 2 * np.pi * prod / L
        s = 1.0 / np.sqrt(64.0 * L)
        return (s * np.cos(a)).astype(np.float32), (-s * np.sin(a)).astype(np.float32)
    C, S = tables(4096, 2048)
    T = np.stack([C, S], 0)
    T = T.reshape(2, 8, 4, 128, 4, 512)
    T = T.transpose(4, 1, 3, 2, 0, 5)
    out["dft"] = np.ascontiguousarray(T).astype(bf)
    Cc, Sc = tables(256, 256)
    Tc = np.stack([Cc, Sc], 0).reshape(2, 2, 128, 256).transpose(2, 1, 0, 3)
    out["dftc"] = np.ascontiguousarray(Tc).astype(bf)
    _FNET_CACHE[hf] = out
    return out


def prep_inputs(inp, l, b, hf, x_b, ctx_b):
    f32 = np.float32
    rev = (hf == 1)
    xs = x_b[::-1] if rev else x_b
    cs = ctx_b[::-1] if rev else ctx_b
    m = {}
    m["xT"] = np.ascontiguousarray(xs.T.astype(f32))
    m["ctxT"] = np.ascontiguousarray(cs.T.astype(f32))
    cc = np.stack([inp["c"][b], inp["c_ctx"]], axis=-1).astype(f32)
    m["cT"] = np.ascontiguousarray(cc.reshape(8, 128, 2).transpose(1, 0, 2).reshape(128, 16))
    m["w_mod"] = np.ascontiguousarray(inp["w_mod"][l])
    m["b_mod2"] = np.ascontiguousarray(np.stack([inp["b_mod"][l]] * 2, 0))
    m["gvecs"] = np.ascontiguousarray(np.stack([fm(inp["g_norm_mix"][l], 8), fm(inp["g_norm_mlp"][l], 8),
                                                fm(inp["b_out"][l], 8), fm(inp["b_ff2"][l], 8)], -1))
    m["w_in"] = np.ascontiguousarray(inp["w_in"][l])
    m["w_out"] = np.ascontiguousarray(inp["w_out"][l])
    m["w_ff1"] = np.ascontiguousarray(inp["w_ff1"][l])
    m["b_ff1"] = fm(inp["b_ff1"][l], 32)
    m["w_ff2"] = np.ascontiguousarray(inp["w_ff2"][l])
    m["g_group"] = fm(inp["g_group"][l], 8)
    m["g_final"] = fm(inp["g_final"], 8)
    wdw = inp["conv_w_dw"][l]
    if rev:
        wdw = wdw[::-1]
    m["conv_wT"] = np.ascontiguousarray(wdw.T.reshape(2, 128, 31).transpose(1, 0, 2).astype(f32))
    m["conv_vec"] = np.ascontiguousarray(np.stack([fm(inp["conv_b_dw"][l], 2), fm(inp["conv_ln_g"][l], 2),
                                                   fm(inp["conv_ln_b"][l], 2), fm(inp["conv_b_pw"][l], 2)], -1))
    m["conv_w_pw"] = np.ascontiguousarray(inp["conv_w_pw"][l])
    m["ident"] = np.eye(128, dtype=f32)
    m.update(fnet_consts(hf))
    m["natab"] = na_table(inp["na_rpb"][l], hf)
    m.update(ssm_layout(inp, l, hf))
    m["fnet_w"] = np.ascontiguousarray(inp["fnet_w"][l])
    m["fnet_b"] = fm(inp["fnet_b"][l], 2)
    return m


def kernel(**inputs):
    inp = {k: np.asarray(v) for k, v in inputs.items()}
    x = np.asarray(inp["x"], np.float32)
    ctx = np.asarray(inp["ctx"], np.float32)
    n_layers = inp["w_in"].shape[0]
    for l in range(n_layers):
        last = (l == n_layers - 1)
        nc = build_layer(ctx_out=not last, final_norm=last)
        in_maps = [prep_inputs(inp, l, core // 2, core % 2, x[core // 2], ctx[core // 2]) for core in range(8)]
        res = run_bass_kernel_spmd(nc, in_maps, core_ids=list(range(8)))
        newx = np.empty_like(x)
        newctx = ctx.copy()
        for core in range(8):
            b, hf = core // 2, core % 2
            xo = np.asarray(res.results[core]["xoT"]).T
            if hf == 1:
                newx[b, NT_OWN:] = xo[::-1]
            else:
                newx[b, :NT_OWN] = xo
                if not last:
                    newctx[b] = np.asarray(res.results[core]["ctxoT"]).T
        x, ctx = newx, newctx
    return np.ascontiguousarray(x.astype(np.float32))
```
